# Optimizing a Trainium2 kernel written in Bass

```python
import jax, jax.numpy as jnp
from jax import lax
import numpy as np

D_MODEL = 1024
BATCH = 8
SEQ = 4096
DEPTH = 1

HEAD_DIM = 64
A_HEADS = 8
A_Q_RANK = 256
A_V_LATENT = 128
IDX_HEADS = 8
IDX_DIM = 64
B_HEADS = 8
D_FF = 2816
MAX_TOPK = 256
Q_BLOCK = 128
ROPE_THETA = 10000.0
EPS = 1e-6
A_WIDTH = A_HEADS * HEAD_DIM
B_WIDTH = B_HEADS * HEAD_DIM
IN_SPLITS = (A_Q_RANK, HEAD_DIM, A_V_LATENT, IDX_DIM, IDX_HEADS,
             B_WIDTH, B_WIDTH, B_WIDTH, D_MODEL, D_MODEL)
IN_COLS = sum(IN_SPLITS)

kernel_name = 'hybrid_dsa_stickbreak_macaron'


def rms_norm(x, g):
    xf = x.astype(jnp.float32)
    y = xf * lax.rsqrt(jnp.mean(xf * xf, axis=-1, keepdims=True) + EPS)
    return (y * g.astype(jnp.float32)).astype(x.dtype)


def swiglu(h, w_gate, w_up, w_down):
    return (jax.nn.silu(h @ w_gate) * (h @ w_up)) @ w_down


def rope_tables(positions, dim):
    inv_freq = ROPE_THETA ** (-jnp.arange(0, dim, 2, dtype=jnp.float32) / dim)
    ang = positions.astype(jnp.float32)[..., None] * inv_freq
    return jnp.cos(ang), jnp.sin(ang)


def apply_rope(x, cos, sin):
    xf = x.astype(jnp.float32)
    x1, x2 = jnp.split(xf, 2, axis=-1)
    return jnp.concatenate([x1 * cos - x2 * sin, x1 * sin + x2 * cos], axis=-1).astype(x.dtype)


def dsa_attention(q_a, k_a, v_a, q_idx, k_idx, w_idx, w_uv, top_k):
    bsz, seq = k_a.shape[0], k_a.shape[1]
    n_blocks = seq // Q_BLOCK
    key_pos = jnp.arange(seq)

    def block(i):
        q0 = i * Q_BLOCK
        qa = lax.dynamic_slice_in_dim(q_a, q0, Q_BLOCK, axis=1)
        qi = lax.dynamic_slice_in_dim(q_idx, q0, Q_BLOCK, axis=1)
        wi = lax.dynamic_slice_in_dim(w_idx, q0, Q_BLOCK, axis=1)
        t_pos = q0 + jnp.arange(Q_BLOCK)
        dots = jnp.einsum('bthd,bsd->bths', qi.astype(jnp.float32), k_idx.astype(jnp.float32))
        score = jnp.einsum('bths,bth->bts', jax.nn.relu(dots), wi.astype(jnp.float32))
        causal = key_pos[None, :] <= t_pos[:, None]
        score = jnp.where(causal[None], score, -jnp.inf)
        _, idx = lax.top_k(score, top_k)
        valid = idx <= t_pos[None, :, None]
        flat = idx.reshape(bsz, Q_BLOCK * top_k)[:, :, None]
        kg = jnp.take_along_axis(k_a, flat, axis=1).reshape(bsz, Q_BLOCK, top_k, HEAD_DIM)
        vg = jnp.take_along_axis(v_a, flat, axis=1).reshape(bsz, Q_BLOCK, top_k, A_V_LATENT)
        logits = jnp.einsum('bthd,btkd->bthk', qa.astype(jnp.float32), kg.astype(jnp.float32)) * (HEAD_DIM ** -0.5)
        logits = jnp.where(valid[:, :, None, :], logits, -jnp.inf)
        p = jax.nn.softmax(logits, axis=-1).astype(v_a.dtype)
        o_lat = jnp.einsum('bthk,btkc->bthc', p, vg)
        o = jnp.einsum('bthc,hcd->bthd', o_lat, w_uv)
        return o.reshape(bsz, Q_BLOCK, A_WIDTH)

    out = lax.map(block, jnp.arange(n_blocks))
    return out.transpose(1, 0, 2, 3).reshape(bsz, seq, A_WIDTH)


def stick_breaking_attention(q_b, k_b, v_b):
    bsz, seq = q_b.shape[0], q_b.shape[1]
    n_blocks = seq // Q_BLOCK
    key_pos = jnp.arange(seq)

    def block(i):
        q0 = i * Q_BLOCK
        qb = lax.dynamic_slice_in_dim(q_b, q0, Q_BLOCK, axis=1)
        t_pos = q0 + jnp.arange(Q_BLOCK)
        z = jnp.einsum('bthd,bshd->bhts', qb.astype(jnp.float32), k_b.astype(jnp.float32)) * (HEAD_DIM ** -0.5)
        strict = (key_pos[None, :] < t_pos[:, None])[None, None]
        log_not = jnp.where(strict, jax.nn.log_sigmoid(-z), 0.0)
        between = lax.cumsum(log_not, axis=3, reverse=True) - log_not
        log_a = jax.nn.log_sigmoid(z) + between
        a = jnp.where(strict, jnp.exp(log_a), 0.0).astype(v_b.dtype)
        o = jnp.einsum('bhts,bshd->bthd', a, v_b)
        return o.reshape(bsz, Q_BLOCK, B_WIDTH)

    out = lax.map(block, jnp.arange(n_blocks))
    return out.transpose(1, 0, 2, 3).reshape(bsz, seq, B_WIDTH)


def setup_inputs(seed: int = 0) -> dict:
    key = jax.random.key(seed)
    ks = jax.random.split(key, 24)
    f32 = jnp.float32

    def w(k, shape, fan_in):
        return jax.random.normal(k, shape, f32) * (fan_in ** -0.5)

    def gain(k, shape):
        return 1.0 + 0.02 * jax.random.normal(k, shape, f32)

    L = DEPTH
    return {
        'x': jax.random.normal(ks[0], (BATCH, SEQ, D_MODEL), f32),
        'positions': jnp.broadcast_to(jnp.arange(SEQ, dtype=jnp.int32), (BATCH, SEQ)),
        'g_ffn1': gain(ks[1], (L, D_MODEL)),
        'w1_gate': w(ks[2], (L, D_MODEL, D_FF), D_MODEL),
        'w1_up': w(ks[3], (L, D_MODEL, D_FF), D_MODEL),
        'w1_down': w(ks[4], (L, D_FF, D_MODEL), D_FF),
        'g_mix': gain(ks[5], (L, D_MODEL)),
        'w_in': w(ks[6], (L, D_MODEL, IN_COLS), D_MODEL),
        'g_cq': gain(ks[7], (L, A_Q_RANK)),
        'w_uq_a': w(ks[8], (L, A_Q_RANK, A_WIDTH), A_Q_RANK),
        'w_q_idx': w(ks[9], (L, A_Q_RANK, IDX_HEADS * IDX_DIM), A_Q_RANK),
        'g_q_a': gain(ks[10], (L, HEAD_DIM)),
        'g_k_a': gain(ks[11], (L, HEAD_DIM)),
        'w_uv_a': w(ks[12], (L, A_HEADS, A_V_LATENT, HEAD_DIM), A_V_LATENT),
        'w_o_a': w(ks[13], (L, A_WIDTH, D_MODEL), A_WIDTH),
        'w_o_b': w(ks[14], (L, B_WIDTH, D_MODEL), B_WIDTH),
        'w_out': w(ks[15], (L, D_MODEL, D_MODEL), D_MODEL),
        'g_ffn2': gain(ks[16], (L, D_MODEL)),
        'w2_gate': w(ks[17], (L, D_MODEL, D_FF), D_MODEL),
        'w2_up': w(ks[18], (L, D_MODEL, D_FF), D_MODEL),
        'w2_down': w(ks[19], (L, D_FF, D_MODEL), D_FF),
    }


def reference(x, positions, g_ffn1, w1_gate, w1_up, w1_down, g_mix, w_in, g_cq, w_uq_a, w_q_idx,
              g_q_a, g_k_a, w_uv_a, w_o_a, w_o_b, w_out, g_ffn2, w2_gate, w2_up, w2_down):
    bsz, seq, _ = x.shape
    top_k = min(MAX_TOPK, seq // 4)
    cos, sin = rope_tables(positions, HEAD_DIM)
    cos_h, sin_h = cos[:, :, None, :], sin[:, :, None, :]
    offsets = [int(o) for o in np.cumsum(IN_SPLITS)[:-1]]
    for l in range(DEPTH):
        x = x + 0.5 * swiglu(rms_norm(x, g_ffn1[l]), w1_gate[l], w1_up[l], w1_down[l])

        h = rms_norm(x, g_mix[l])
        proj = h @ w_in[l]
        (c_q, k_a, v_a, k_idx, w_idx, q_b, k_b, v_b, gate_a, gate_b) = jnp.split(proj, offsets, axis=-1)

        c_q = rms_norm(c_q, g_cq[l])
        q_a = (c_q @ w_uq_a[l]).reshape(bsz, seq, A_HEADS, HEAD_DIM)
        q_a = apply_rope(rms_norm(q_a, g_q_a[l]), cos_h, sin_h)
        k_a = apply_rope(rms_norm(k_a, g_k_a[l]), cos, sin)
        q_idx = apply_rope((c_q @ w_q_idx[l]).reshape(bsz, seq, IDX_HEADS, IDX_DIM), cos_h, sin_h)
        k_idx = apply_rope(k_idx, cos, sin)
        w_idx = w_idx * ((IDX_HEADS ** -0.5) * (IDX_DIM ** -0.5))
        y_a = dsa_attention(q_a, k_a, v_a, q_idx, k_idx, w_idx, w_uv_a[l], top_k)

        y_b = stick_breaking_attention(q_b.reshape(bsz, seq, B_HEADS, HEAD_DIM),
                                       k_b.reshape(bsz, seq, B_HEADS, HEAD_DIM),
                                       v_b.reshape(bsz, seq, B_HEADS, HEAD_DIM))

        merged = jax.nn.sigmoid(gate_a) * (y_a @ w_o_a[l]) + jax.nn.sigmoid(gate_b) * (y_b @ w_o_b[l])
        x = x + merged @ w_out[l]

        x = x + 0.5 * swiglu(rms_norm(x, g_ffn2[l]), w2_gate[l], w2_up[l], w2_down[l])
    return x
```

```python
import numpy as np
import concourse.bass as bass
import concourse.mybir as mybir
from concourse.bass_utils import run_bass_kernel_spmd

F32 = mybir.dt.float32
F32R = mybir.dt.float32r
BF16 = mybir.dt.bfloat16
I32 = mybir.dt.int32
AF = mybir.ActivationFunctionType
ALU = mybir.AluOpType
AX = mybir.AxisListType

ENGS = ("pe", "act", "dve", "pool", "sp")


class Buf:
    __slots__ = ("name", "writers", "readers", "sem", "cnt")

    def __init__(self, name):
        self.name = name
        self.writers = []
        self.readers = []
        self.sem = None
        self.cnt = 0


class Op:
    __slots__ = ("eng", "fn", "deps", "dma", "slot", "tok_val", "needed", "seq", "idx")

    def __init__(self, eng, fn):
        self.eng = eng
        self.fn = fn
        self.deps = []
        self.dma = False
        self.slot = None
        self.tok_val = 0
        self.needed = False
        self.seq = 0
        self.idx = 0


class Sched:
    def __init__(self, nc):
        self.nc = nc
        self.ops = {e: [] for e in ENGS}
        self.n = 0
        self.slots = []
        self.dmas_since = []
        self.pending = {}

    def _track(self, op, reads, writes, accum=()):
        deps = []
        for b in reads:
            deps.extend(b.writers)
        for b in writes:
            deps.extend(b.writers)
            deps.extend(b.readers)
        for b in accum:
            deps.extend(b.readers)
        seen = set()
        last = {}
        for d in deps:
            if d is op or id(d) in seen:
                continue
            if d.eng == "pe" and op.eng == "pe":
                continue
            seen.add(id(d))
            if (not d.dma) and d.eng in ("pe", "act", "dve"):
                if d.eng not in last or last[d.eng].idx < d.idx:
                    last[d.eng] = d
                continue
            op.deps.append(d)
            d.needed = True
        for d in last.values():
            op.deps.append(d)
            d.needed = True
        for b in reads:
            b.readers.append(op)
        for b in writes:
            b.writers = [op]
            b.readers = []
        for b in accum:
            b.writers.append(op)
            b.readers = []

    def barrier(self):
        deps = [self.ops[e][-1] for e in ENGS if self.ops[e]] + list(self.dmas_since)
        for e in ENGS:
            self.pending[e] = list(self.pending.get(e, [])) + deps
        self.dmas_since = []

    def _apply_pending(self, op):
        p = self.pending.pop(op.eng, None)
        if p:
            seen = set(id(d) for d in op.deps)
            for d in p:
                if d is op or id(d) in seen:
                    continue
                if d.eng == "pe" and op.eng == "pe" and not d.dma:
                    continue
                seen.add(id(d))
                op.deps.append(d)
                d.needed = True

    def begin_capture(self):
        self.cap = []

    def end_capture(self):
        c, self.cap = self.cap, None
        return c

    def replay(self, item):
        if item[0] == "add":
            self.add(*item[1:])
        else:
            self.dma(*item[1:])

    def add(self, eng, fn, reads=(), writes=(), accum=()):
        if getattr(self, "cap", None) is not None:
            self.cap.append(("add", eng, fn, tuple(reads), tuple(writes), tuple(accum)))
            return None
        op = Op(eng, fn)
        op.idx = self.n
        self.n += 1
        self._track(op, reads, writes, accum)
        self._apply_pending(op)
        self.ops[eng].append(op)
        return op

    def dma(self, eng, fn, slot, reads=(), writes=(), accum=()):
        if getattr(self, "cap", None) is not None:
            self.cap.append(("dma", eng, fn, slot, tuple(reads), tuple(writes), tuple(accum)))
            return None
        op = Op(eng, fn)
        op.idx = self.n
        self.n += 1
        op.dma = True
        op.slot = slot
        if slot.sem is None:
            slot.sem = True
            self.slots.append(slot)
        slot.cnt += 16
        op.tok_val = slot.cnt
        self._track(op, reads, writes, accum)
        self._apply_pending(op)
        self.dmas_since.append(op)
        self.ops[eng].append(op)
        return op

    def emit(self, final_waits=()):
        nc = self.nc
        import contextlib
        with contextlib.ExitStack() as es:
            es.enter_context(nc.allow_low_precision("bf16/fp32r matmul operands by design"))
            esem = {e: es.enter_context(nc.semaphore("s_" + e)) for e in ENGS}
            for s in self.slots:
                s.sem = es.enter_context(nc.semaphore("d_" + s.name))
            for e in ENGS:
                k = 0
                for op in self.ops[e]:
                    if not op.dma and op.needed:
                        k += 1
                        op.seq = k
            block = es.enter_context(nc.Block())

            def tok(op):
                if op.dma:
                    return op.slot.sem, op.tok_val
                return esem[op.eng], op.seq

            def run(e, engine):
                waited = {}
                for op in self.ops[e]:
                    need = {}
                    for d in op.deps:
                        s, v = tok(d)
                        key = s.name
                        if waited.get(key, 0) >= v:
                            continue
                        if key not in need or need[key][1] < v:
                            need[key] = (s, v)
                    for key, (s, v) in need.items():
                        waited[key] = v
                        engine.wait_ge(s, v)
                    ins = op.fn(engine)
                    if op.dma:
                        ins.then_inc(op.slot.sem, 16)
                    elif op.needed:
                        ins.then_inc(esem[e], 1)
                if e == "sp":
                    for b in final_waits:
                        engine.wait_ge(b.sem, b.cnt)

            @block.tensor
            def _(eng):
                run("pe", eng)

            @block.scalar
            def _(eng):
                run("act", eng)

            @block.vector
            def _(eng):
                run("dve", eng)

            @block.gpsimd
            def _(eng):
                run("pool", eng)

            @block.sync
            def _(eng):
                run("sp", eng)


def MM(out, lhsT, rhs, start, stop):
    return lambda e: e.matmul(out, lhsT=lhsT, rhs=rhs, start=start, stop=stop)


def TR(out, in_, ident):
    return lambda e: e.transpose(out, in_, ident)


def ACT(out, in_, func, **kw):
    return lambda e: e.activation(out=out, in_=in_, func=func, **kw)


def TT(out, in0, in1, op):
    return lambda e: e.tensor_tensor(out=out, in0=in0, in1=in1, op=op)


def TS(out, in0, s1, s2, op0, op1=None, **kw):
    if op1 is None:
        return lambda e: e.tensor_scalar(out=out, in0=in0, scalar1=s1, scalar2=s2, op0=op0, **kw)
    return lambda e: e.tensor_scalar(out=out, in0=in0, scalar1=s1, scalar2=s2, op0=op0, op1=op1, **kw)


def STT(out, in0, scalar, in1, op0, op1):
    return lambda e: e.scalar_tensor_tensor(out=out, in0=in0, scalar=scalar, in1=in1, op0=op0, op1=op1)


def CP(out, in_):
    return lambda e: e.tensor_copy(out=out, in_=in_)


def ACP(out, in_):
    return lambda e: e.copy(out=out, in_=in_)


def DMA(out, in_, slow=False):
    if slow:
        return lambda e: e.dma_start(out=out, in_=in_, allow_slow_non_contiguous=True)
    return lambda e: e.dma_start(out=out, in_=in_)


def RED(out, in_, op, **kw):
    return lambda e: e.tensor_reduce(out=out, in_=in_, axis=AX.X, op=op, **kw)


SEQ = 4096
DM = 1024
DFF = 2816
NFC = DFF // 128
NTT = SEQ // 512
NTB = SEQ // 128
INC = 4104
EPS = 1e-6
TOPK = 256
NBIS = 18
DUM4 = 0
DUM2 = 0


import contextlib


class WStream:
    def __init__(self, k, bufs):
        self.k = k
        self.bufs = bufs
        self.items = []
        self.i = 0
        self.j = 0

    def push(self, src3):
        self.items.append(src3)

    def _issue(self):
        src = self.items[self.j]
        ap, B = self.bufs[self.j % len(self.bufs)]
        kc, w = src.shape[1], src.shape[2]
        self.k.S.dma("pool", DMA(ap[:, 0:kc, 0:w], src), B, writes=[B])
        self.j += 1

    def get(self):
        n = len(self.bufs)
        while self.j < len(self.items) and self.j < self.i + n - 1:
            self._issue()
        ap, B = self.bufs[self.i % n]
        self.i += 1
        return ap, B


class K:
    def __init__(self, debug=False):
        self.debug = debug
        self.nc = bass.Bass("TRN2", target_bir_lowering=False)
        self.S = Sched(self.nc)
        self.es = contextlib.ExitStack()
        self.bi = 0
        self.rr = 0
        self.outs = {}

    def din(self, name, shape, dt=F32):
        return self.nc.dram_tensor(name, list(shape), dt, kind="ExternalInput").ap()

    def dscratch(self, name, shape, dt):
        kind = "ExternalOutput" if self.debug else "Internal"
        ap = self.nc.dram_tensor(name, list(shape), dt, kind=kind).ap()
        return ap, Buf(name)

    def sb(self, es, name, shape, dt=F32):
        t = es.enter_context(self.nc.sbuf_tensor(name, list(shape), dt))
        return t, Buf(name)

    def alloc_banks(self):
        self.banks = []
        self.pairs = []
        for i in range(4):
            t = self.es.enter_context(self.nc.psum_tensor("pp%d" % i, [128, 2, 512], F32))
            self.pairs.append(t)
            for j in range(2):
                self.banks.append((t[:, j, :], Buf("ps%d" % (2 * i + j))))

    def bank(self):
        pool = getattr(self, "pool", None) or list(range(8))
        self.bi += 1
        return self.banks[pool[self.bi % len(pool)]]

    def dummies(self, n, bank):
        zl, Bzl = self.c["zl"]
        zr, Bzr = self.c["zr"]
        b, Bb = bank
        for _ in range(n):
            self.S.add("pe", MM(b[:, :], zl[:, :], zr[:, :], False, False), reads=[Bzl, Bzr], accum=[Bb])

    def ev(self):
        self.rr ^= 1
        return "act" if self.rr else "dve"

    def copy(self, eng, out, in_, reads, writes):
        if eng == "act":
            self.S.add("act", ACP(out, in_), reads=reads, writes=writes)
        else:
            self.S.add(eng, CP(out, in_), reads=reads, writes=writes)


def emit_rstd(k, rs, Brs, scale):
    S = k.S
    S.add("dve", TS(rs, rs, scale, EPS, ALU.mult, ALU.add), reads=[Brs], writes=[Brs])
    S.add("act", ACT(rs, rs, AF.Sqrt), reads=[Brs], writes=[Brs])
    S.add("dve", lambda e: e.reciprocal(out=rs, in_=rs), reads=[Brs], writes=[Brs])


def emit_norm_T(k, c, xt, Bxt, g_bc, Bg):
    S = k.S
    hbf, Bhbf = c["hbf"]
    hT, BhT = c["hT"]
    ss, Bss = c["ss"]
    ident, Bid = c["ident"]
    S.add("pool", lambda e: e.memset(ss[:, 0:4], 0.0), writes=[Bss])
    for j in range(4):
        S.add("act", ACT(hbf[:, j, :], xt[:, j, :], AF.Square, accum_out=ss[:, j:j + 1]),
              reads=[Bxt], writes=[Bhbf, Bss])
    emit_rstd(k, ss[:, 0:4], Bss, 1.0 / DM)
    for j in range(4):
        S.add("dve", STT(hbf[:, j, :], xt[:, j, :], ss[:, j:j + 1], g_bc[:, :], ALU.mult, ALU.mult),
              reads=[Bxt, Bss, Bg], writes=[Bhbf])
    for kc in range(8):
        pb, Bpb = k.bank()
        pbf = pb[:].bitcast(BF16)
        for j in range(4):
            S.add("pe", TR(pbf[:, j * 128:(j + 1) * 128], hbf[:, j, kc * 128:(kc + 1) * 128], ident[:, :]),
                  reads=[Bhbf, Bid], writes=[Bpb] if j == 0 else (), accum=() if j == 0 else [Bpb])
        k.copy(k.ev(), hT[:, kc, :], pbf[:, 0:512], [Bpb], [BhT])


def emit_ffn(k, c, xt, Bxt, g_bc, Bg, ws, wd, Bwd):
    S = k.S
    emit_norm_T(k, c, xt, Bxt, g_bc, Bg)
    hT, BhT = c["hT"]
    aT, BaT = c["aT"]
    groups = [(0, 4), (4, 4), (8, 4), (12, 4), (16, 4), (20, 2)]
    for (f0, nf) in groups:
        wg, Bwg = ws.get()
        wu, Bwu = ws.get()
        for fi in range(nf):
            fc = f0 + fi
            pg, Bpg = k.bank()
            pu, Bpu = k.bank()
            for kc in range(8):
                S.add("pe", MM(pg[:, :], wg[:, kc, fi * 128:(fi + 1) * 128], hT[:, kc, :], kc == 0, kc == 7),
                      reads=[Bwg, BhT], writes=[Bpg] if kc == 0 else (), accum=() if kc == 0 else [Bpg])
            for kc in range(8):
                S.add("pe", MM(pu[:, :], wu[:, kc, fi * 128:(fi + 1) * 128], hT[:, kc, :], kc == 0, kc == 7),
                      reads=[Bwu, BhT], writes=[Bpu] if kc == 0 else (), accum=() if kc == 0 else [Bpu])
            sg, Bsg = c["sg"][fc % 2]
            S.add("act", ACT(sg[:, :], pg[:, :], AF.Silu), reads=[Bpg], writes=[Bsg])
            S.add("dve", TT(aT[:, fc, :], sg[:, :], pu[:, :], ALU.mult), reads=[Bsg, Bpu], writes=[BaT[fc]])
    for j in range(4):
        for half in range(2):
            po, Bpo = k.bank()
            for fc in range(NFC):
                S.add("pe", MM(po[:, :], aT[:, fc, j * 128:(j + 1) * 128], wd[:, fc, half * 512:(half + 1) * 512],
                               fc == 0, fc == NFC - 1),
                      reads=[BaT[fc], Bwd], writes=[Bpo] if fc == 0 else (), accum=() if fc == 0 else [Bpo])
            sl = xt[:, j, half * 512:(half + 1) * 512]
            S.add("dve", STT(sl, po[:, :], 0.5, sl, ALU.mult, ALU.add), reads=[Bpo, Bxt], writes=[Bxt])


def push_ffn_weights(ws, Wg_r, Wu_r):
    for (f0, nf) in [(0, 4), (4, 4), (8, 4), (12, 4), (16, 4), (20, 2)]:
        ws.push(Wg_r[:, :, f0 * 128:(f0 + nf) * 128])
        ws.push(Wu_r[:, :, f0 * 128:(f0 + nf) * 128])


def emit_rope(k, eng, out1, out2, x1, x2, cosb, sinb, t, Bt, reads, writes):
    S = k.S
    t1, t2, t3, t4 = t
    S.add(eng, TT(t1, x1, cosb, ALU.mult), reads=reads, writes=[Bt])
    S.add(eng, TT(t2, x2, sinb, ALU.mult), reads=reads + [Bt], writes=[Bt])
    S.add(eng, TT(t3, x1, sinb, ALU.mult), reads=reads + [Bt], writes=[Bt])
    S.add(eng, TT(t4, x2, cosb, ALU.mult), reads=reads + [Bt], writes=[Bt])
    S.add(eng, TT(out1, t1, t2, ALU.subtract), reads=[Bt], writes=writes)
    S.add(eng, TT(out2, t3, t4, ALU.add), reads=[Bt], writes=writes)


def phase1(k, D):
    nc, S = k.nc, k.S
    I = k.inp
    with contextlib.ExitStack() as pes:
        sb = lambda name, shape, dt=F32: k.sb(pes, name, shape, dt)
        c = k.c
        c["hbf"] = sb("hbf", [128, 4, DM], BF16)
        c["hT"] = sb("hT", [128, 8, 512], BF16)
        aT, _ = sb("aT", [128, NFC, 512], BF16)
        c["aT"] = (aT, [Buf("aT%d" % i) for i in range(NFC)])
        c["sg"] = [sb("sg0", [128, 512], BF16), sb("sg1", [128, 512], BF16)]
        c["ss"] = sb("ss", [128, 4])
        xt, Bxt = sb("xt", [128, 4, DM])
        wd, Bwd = sb("wd", [128, NFC, DM], BF16)
        ws = WStream(k, [sb("wb%d" % i, [128, 8, 520], BF16) for i in range(4)])
        w_uq, Bwuq = sb("w_uq", [128, 2, 512], BF16)
        w_qi, Bwqi = sb("w_qi", [128, 2, 512], BF16)
        w_uv, Bwuv = sb("w_uv", [128, 512], BF16)
        gq_bc, Bgq = sb("gq_bc", [128, 512])
        gk_bc, Bgk = sb("gk_bc", [128, 64])
        gcq, Bgcq = sb("gcq", [128, 2])
        cq32, Bcq32 = sb("cq32", [128, 2, 512])
        sqb, Bsqb = sb("sqb", [128, 2, 512], BF16)
        rbc, Brbc = sb("rbc", [128, 512])
        cqn, Bcqn = sb("cqn", [128, 2, 512], BF16)
        qn, Bqn = sb("qn", [128, 512])
        qsq, Bqsq = sb("qsq", [128, 512])
        r8, Br8 = sb("r8", [128, 8])
        tq, Btq = sb("tq", [128, 4, 256])
        qr, Bqr = sb("qr", [128, 512], BF16)
        qi_r, Bqir = sb("qi_r", [128, 512], BF16)
        tq2, Btq2 = sb("tq2", [128, 4, 256])
        qaT_st, BqaT = sb("qaT_st", [128, 4, 512], BF16)
        qiT_st, BqiT = sb("qiT_st", [128, 4, 512], BF16)
        kk, Bkk = sb("kk", [128, 2, 128], BF16)
        kn, Bkn = sb("kn", [128, 64])
        tk, Btk = sb("tk", [128, 4, 32])
        r1, Br1 = sb("r1", [128, 1])
        kaT_st, BkaT = sb("kaT_st", [128, 512], BF16)
        kiT_st, BkiT = sb("kiT_st", [128, 512], BF16)
        wi_st, Bwi = sb("wi_st", [128, 4, 8])
        vaT, BvaT = sb("vaT", [128, 512], BF16)
        vh_st, Bvh = sb("vh_st", [128, 4, 8, 65], BF16)
        ch_st = [sb("ch_st%d" % i, [128, 512], BF16) for i in range(3)]
        vb_st, Bvb = sb("vb_st", [128, 4, 512], BF16)
        ident, Bid = c["ident"]
        ones_bf, Bones = c["ones_bf"]
        cos, sin, Bcs = c["cos"], c["sin"], c["Bcs"]
        g1_bc, Bg1 = c["g1_bc"]
        gm_bc, Bgm = c["gm_bc"]

        Wd_r = I["w1_down"].rearrange("(fc p) d -> p fc d", p=128)
        for q in range(0, NFC, 6):
            n = min(6, NFC - q)
            S.dma("pool", DMA(wd[:, q:q + n, :], Wd_r[:, q:q + n, :]), Bwd, accum=[Bwd])
        S.dma("pool", DMA(w_uq[:, :, :], I["w_uq_a"].rearrange("(kc p) f -> p kc f", p=128)), Bwuq, writes=[Bwuq])
        S.dma("pool", DMA(w_qi[:, :, :], I["w_q_idx"].rearrange("(kc p) f -> p kc f", p=128)), Bwqi, writes=[Bwqi])
        S.dma("pool", DMA(w_uv[:, :].rearrange("c (h d) -> c h d", h=8), I["w_uv_a"].rearrange("h c d -> c h d")), Bwuv, writes=[Bwuv])
        S.dma("sp", DMA(gq_bc[:, :].rearrange("p (h d) -> p h d", h=8),
                        I["g_q_a"].rearrange("(o h) d -> o h d", o=1).to_broadcast([128, 8, 64])), Bgq, writes=[Bgq])
        S.dma("sp", DMA(gk_bc[:, :], I["g_k_a"].to_broadcast([128, 64])), Bgk, writes=[Bgk])
        S.dma("sp", DMA(gcq[:, :], I["g_cq"].rearrange("o (kc p) -> p (o kc)", p=128), slow=True), Bgcq, writes=[Bgcq])
        S.add("pool", lambda e: e.memset(vh_st[:, :, :, 64:65], 1.0), writes=[Bvh])

        Wg_r = I["w1_gate"].rearrange("(kc p) f -> p kc f", p=128)
        Wu_r = I["w1_up"].rearrange("(kc p) f -> p kc f", p=128)
        Win_r = I["w_in"].rearrange("(kc p) f -> p kc f", p=128)
        pieces = [(0, 520), (520, 512), (1032, 512), (1544, 512), (2056, 512), (2568, 512), (3080, 512), (3592, 512)]
        for tt in range(NTT):
            push_ffn_weights(ws, Wg_r, Wu_r)
            for (c0, w) in pieces:
                ws.push(Win_r[:, :, c0:c0 + w])

        x_r = I["x"].rearrange("(b p) d -> p b d", p=128)
        for tt in range(NTT):
            t0 = tt * 512
            S.dma("sp", DMA(xt[:, :, :], x_r[:, tt * 4:(tt + 1) * 4, :]), Bxt, writes=[Bxt])
            emit_ffn(k, c, xt, Bxt, g1_bc, Bg1, ws, wd, Bwd)
            S.dma("sp", DMA(D["x1"][0].rearrange("(b p) d -> p b d", p=128)[:, tt * 4:(tt + 1) * 4, :], xt[:, :, :]),
                  Bxt, reads=[Bxt], accum=[D["x1"][1]])
            emit_norm_T(k, c, xt, Bxt, gm_bc, Bgm)
            hT, BhT = c["hT"]
            S.begin_capture()
            k.pool = None
            wA, BwA = ws.get()
            pcs = []
            for ci in range(2):
                pc, Bpc = k.bank()
                for kc in range(8):
                    S.add("pe", MM(pc[:, :], wA[:, kc, ci * 128:(ci + 1) * 128], hT[:, kc, :], kc == 0, kc == 7),
                          reads=[BwA, BhT], writes=[Bpc] if kc == 0 else (), accum=() if kc == 0 else [Bpc])
                S.add("act", ACP(cq32[:, ci, :], pc[:, :]), reads=[Bpc], writes=[Bcq32])
                S.add("act", ACT(sqb[:, ci, :], pc[:, :], AF.Square), reads=[Bpc], writes=[Bsqb])
            pv, Bpv = k.bank()
            for kc in range(8):
                S.add("pe", MM(pv[:, :], wA[:, kc, 320:448], hT[:, kc, :], kc == 0, kc == 7),
                      reads=[BwA, BhT], writes=[Bpv] if kc == 0 else (), accum=() if kc == 0 else [Bpv])
            k.copy("dve", vaT[:, :], pv[:, :], [Bpv], [BvaT])
            pss, Bpss = k.bank()
            for ci in range(2):
                S.add("pe", MM(pss[:, :], ones_bf[:, :], sqb[:, ci, :], ci == 0, ci == 1),
                      reads=[Bones, Bsqb], writes=[Bpss] if ci == 0 else (), accum=() if ci == 0 else [Bpss])
            S.add("dve", TS(rbc[:, :], pss[:, :], 1.0 / 256, EPS, ALU.mult, ALU.add), reads=[Bpss], writes=[Brbc])
            S.add("act", ACT(rbc[:, :], rbc[:, :], AF.Sqrt), reads=[Brbc], writes=[Brbc])
            S.add("dve", lambda e: e.reciprocal(out=rbc[:, :], in_=rbc[:, :]), reads=[Brbc], writes=[Brbc])
            for ci in range(2):
                S.add("dve", STT(cqn[:, ci, :], cq32[:, ci, :], gcq[:, ci:ci + 1], rbc[:, :], ALU.mult, ALU.mult),
                      reads=[Bcq32, Bgcq, Brbc], writes=[Bcqn])
            for j in range(4):
                b = tt * 4 + j
                ts_ = slice(j * 128, (j + 1) * 128)
                cosb = cos[:, b, :].unsqueeze(1).to_broadcast([128, 8, 32])
                sinb = sin[:, b, :].unsqueeze(1).to_broadcast([128, 8, 32])
                pqa, Bpqa = k.bank()
                for ci in range(2):
                    S.add("pe", MM(pqa[:, :], cqn[:, ci, ts_], w_uq[:, ci, :], ci == 0, ci == 1),
                          reads=[Bcqn, Bwuq], writes=[Bpqa] if ci == 0 else (), accum=() if ci == 0 else [Bpqa])
                pqi, Bpqi = k.bank()
                for ci in range(2):
                    S.add("pe", MM(pqi[:, :], cqn[:, ci, ts_], w_qi[:, ci, :], ci == 0, ci == 1),
                          reads=[Bcqn, Bwqi], writes=[Bpqi] if ci == 0 else (), accum=() if ci == 0 else [Bpqi])
                S.add("act", ACT(qsq[:, :], pqa[:, :], AF.Square), reads=[Bpqa], writes=[Bqsq])
                S.add("dve", RED(r8[:, :], qsq[:, :].rearrange("p (h d) -> p h d", h=8), ALU.add), reads=[Bqsq], writes=[Br8])
                emit_rstd(k, r8[:, :], Br8, 1.0 / 64)
                S.add("dve", TT(qn[:, :].rearrange("p (h d) -> p h d", h=8), pqa[:, :].rearrange("p (h d) -> p h d", h=8),
                                r8[:, :].unsqueeze(2).to_broadcast([128, 8, 64]), ALU.mult),
                      reads=[Bpqa, Br8], writes=[Bqn])
                S.add("dve", TT(qn[:, :], qn[:, :], gq_bc[:, :], ALU.mult), reads=[Bqn, Bgq], writes=[Bqn])
                qn3 = qn[:, :].rearrange("p (h d) -> p h d", h=8)
                qr3 = qr[:, :].rearrange("p (h d) -> p h d", h=8)
                tqs = [tq[:, i, :].rearrange("p (h d) -> p h d", h=8) for i in range(4)]
                emit_rope(k, "dve", qr3[:, :, 0:32], qr3[:, :, 32:64], qn3[:, :, 0:32], qn3[:, :, 32:64], cosb, sinb,
                          tqs, Btq, [Bqn, Bcs], [Bqr])
                S.add("act", ACP(qsq[:, :], pqi[:, :]), reads=[Bpqi], writes=[Bqsq])
                qs3 = qsq[:, :].rearrange("p (h d) -> p h d", h=8)
                qi3 = qi_r[:, :].rearrange("p (h d) -> p h d", h=8)
                tq2s = [tq2[:, i, :].rearrange("p (h d) -> p h d", h=8) for i in range(4)]
                emit_rope(k, "pool", qi3[:, :, 0:32], qi3[:, :, 32:64], qs3[:, :, 0:32], qs3[:, :, 32:64], cosb, sinb,
                          tq2s, Btq2, [Bqsq, Bcs], [Bqir])
                for (src, Bsrc, dst, Bdst) in ((qr, Bqr, qaT_st, BqaT), (qi_r, Bqir, qiT_st, BqiT)):
                    pb, Bpb = k.bank()
                    pbf = pb[:].bitcast(BF16)
                    for pr in range(4):
                        S.add("pe", TR(pbf[:, pr * 128:(pr + 1) * 128], src[:, pr * 128:(pr + 1) * 128], ident[:, :]),
                              reads=[Bsrc, Bid], writes=[Bpb] if pr == 0 else (), accum=() if pr == 0 else [Bpb])
                    k.copy(k.ev(), dst[:, :, ts_], pbf[:, 0:512].rearrange("p (a t) -> p a t", a=4), [Bpb], [Bdst])
                pk, Bpk = k.bank()
                for kc in range(8):
                    S.add("pe", MM(pk[:, 0:264], hT[:, kc, ts_], wA[:, kc, 256:520], kc == 0, kc == 7),
                          reads=[BwA, BhT], writes=[Bpk] if kc == 0 else (), accum=() if kc == 0 else [Bpk])
                S.add("pool", lambda e: e.memset(r1[:, :], 0.0), writes=[Br1])
                S.add("act", ACT(kn[:, :], pk[:, 0:64], AF.Square, accum_out=r1[:, 0:1]), reads=[Bpk], writes=[Bkn, Br1])
                emit_rstd(k, r1[:, :], Br1, 1.0 / 64)
                S.add("dve", STT(kn[:, :], pk[:, 0:64], r1[:, 0:1], gk_bc[:, :], ALU.mult, ALU.mult),
                      reads=[Bpk, Br1, Bgk], writes=[Bkn])
                tks = [tk[:, i, :] for i in range(4)]
                emit_rope(k, "dve", kk[:, 0, 0:32], kk[:, 0, 32:64], kn[:, 0:32], kn[:, 32:64], cos[:, b, :], sin[:, b, :],
                          tks, Btk, [Bkn, Bcs], [Bkk])
                emit_rope(k, "dve", kk[:, 1, 0:32], kk[:, 1, 32:64], pk[:, 192:224], pk[:, 224:256], cos[:, b, :], sin[:, b, :],
                          tks, Btk, [Bpk, Bcs], [Bkk])
                S.add("dve", CP(kk[:, :, 64:128], kk[:, :, 0:64]), reads=[Bkk], writes=[Bkk])
                S.add("dve", TS(wi_st[:, j, :], pk[:, 256:264], 1.0 / (8.0 ** 0.5 * 8.0), None, ALU.mult), reads=[Bpk], writes=[Bwi])
                pb, Bpb = k.bank()
                pbf = pb[:].bitcast(BF16)
                for i2 in range(2):
                    S.add("pe", TR(pbf[:, i2 * 128:(i2 + 1) * 128], kk[:, i2, :], ident[:, :]),
                          reads=[Bkk, Bid], writes=[Bpb] if i2 == 0 else (), accum=() if i2 == 0 else [Bpb])
                k.copy("act", kaT_st[:, ts_], pbf[:, 0:128], [Bpb], [BkaT])
                k.copy("act", kiT_st[:, ts_], pbf[:, 128:256], [Bpb], [BkiT])
                pvh, Bpvh = k.bank()
                S.add("pe", MM(pvh[:, :], vaT[:, ts_], w_uv[:, :], True, True), reads=[BvaT, Bwuv], writes=[Bpvh])
                k.copy("act", vh_st[:, j, :, 0:64], pvh[:, :].rearrange("p (h d) -> p h d", h=8), [Bpvh], [Bvh])
            S.dma("sp", DMA(D["qaT"][0][:, :, t0:t0 + 512].rearrange("a p t -> p a t"), qaT_st[:, :, :]), BqaT, reads=[BqaT], accum=[D["qaT"][1]])
            S.dma("sp", DMA(D["qiT"][0][:, :, t0:t0 + 512].rearrange("a p t -> p a t"), qiT_st[:, :, :]), BqiT, reads=[BqiT], accum=[D["qiT"][1]])
            S.dma("sp", DMA(D["kaT"][0][:, t0:t0 + 512], kaT_st[:, :]), BkaT, reads=[BkaT], accum=[D["kaT"][1]])
            S.dma("sp", DMA(D["kiT"][0][:, t0:t0 + 512], kiT_st[:, :]), BkiT, reads=[BkiT], accum=[D["kiT"][1]])
            S.dma("sp", DMA(D["widx"][0].rearrange("(b p) h -> p b h", p=128)[:, tt * 4:(tt + 1) * 4, :], wi_st[:, :, :]), Bwi, reads=[Bwi], accum=[D["widx"][1]])
            S.dma("sp", DMA(D["vh"][0].rearrange("(b p) h d -> p b h d", p=128)[:, tt * 4:(tt + 1) * 4, :, :], vh_st[:, :, :, :]), Bvh, reads=[Bvh], accum=[D["vh"][1]])
            X = S.end_capture()
            S.begin_capture()
            k.pool = None
            for (nm, scl) in (("qbT", 0.125), ("kbT", 1.0)):
                wB, BwB = ws.get()
                for pr in range(4):
                    pq, Bpq = k.bank()
                    for kc in range(8):
                        S.add("pe", MM(pq[:, :], wB[:, kc, pr * 128:(pr + 1) * 128], hT[:, kc, :], kc == 0, kc == 7),
                              reads=[BwB, BhT], writes=[Bpq] if kc == 0 else (), accum=() if kc == 0 else [Bpq])
                    st, Bst = ch_st[k.chi % 3]
                    k.chi += 1
                    S.add("act", ACT(st[:, :], pq[:, :], AF.Copy, scale=scl), reads=[Bpq], writes=[Bst])
                    S.dma("sp", DMA(D[nm][0][pr, :, t0:t0 + 512], st[:, :]), Bst, reads=[Bst], accum=[D[nm][1]])
            wDp, BwD = ws.get()
            for j in range(4):
                pq, Bpq = k.bank()
                for kc in range(8):
                    S.add("pe", MM(pq[:, :], hT[:, kc, j * 128:(j + 1) * 128], wDp[:, kc, 0:512], kc == 0, kc == 7),
                          reads=[BwD, BhT], writes=[Bpq] if kc == 0 else (), accum=() if kc == 0 else [Bpq])
                k.copy(k.ev(), vb_st[:, j, :], pq[:, :], [Bpq], [Bvb])
            S.dma("sp", DMA(D["vb"][0].rearrange("(b p) f -> p b f", p=128)[:, tt * 4:(tt + 1) * 4, :], vb_st[:, :, :]), Bvb, reads=[Bvb], accum=[D["vb"][1]])
            for gi, nm in enumerate(("gaT", "gaT", "gbT", "gbT")):
                wG, BwG = ws.get()
                for mc in range(4):
                    pq, Bpq = k.bank()
                    for kc in range(8):
                        S.add("pe", MM(pq[:, :], wG[:, kc, mc * 128:(mc + 1) * 128], hT[:, kc, :], kc == 0, kc == 7),
                              reads=[BwG, BhT], writes=[Bpq] if kc == 0 else (), accum=() if kc == 0 else [Bpq])
                    st, Bst = ch_st[k.chi % 3]
                    k.chi += 1
                    S.add("act", ACT(st[:, :], pq[:, :], AF.Sigmoid), reads=[Bpq], writes=[Bst])
                    m0 = ((gi % 2) * 4 + mc) * 128
                    S.dma("sp", DMA(D[nm][0][m0:m0 + 128, t0:t0 + 512], st[:, :]), Bst, reads=[Bst], accum=[D[nm][1]])
            Y = S.end_capture()
            k.pool = None
            ix = iy = 0
            nx, ny = len(X), len(Y)
            for it_ in X:
                S.replay(it_)
            for it_ in Y:
                S.replay(it_)
        S.barrier()


WEIGHT_SPECS = [
    ("g_ffn1", [1, DM]), ("w1_gate", [DM, DFF]), ("w1_up", [DM, DFF]), ("w1_down", [DFF, DM]),
    ("g_mix", [1, DM]), ("w_in", [DM, INC]), ("g_cq", [1, 256]), ("w_uq_a", [256, 512]),
    ("w_q_idx", [256, 512]), ("g_q_a", [1, 64]), ("g_k_a", [1, 64]), ("w_uv_a", [8, 128, 64]),
    ("w_o_a", [512, DM]), ("w_o_b", [512, DM]), ("w_out", [DM, DM]), ("g_ffn2", [1, DM]),
    ("w2_gate", [DM, DFF]), ("w2_up", [DM, DFF]), ("w2_down", [DFF, DM]),
]


def host_consts():
    ident = np.eye(128, dtype=np.float32)
    ones = np.ones((128, 128), dtype=np.float32)
    j = np.arange(128)
    tri = (j[:, None] >= j[None, :]).astype(np.float32)
    invf = (10000.0 ** (-np.arange(0, 64, 2, dtype=np.float32) / 64)).astype(np.float32)
    invf_bc = np.broadcast_to(invf[None, :], (128, 32)).copy()
    mstrict = (j[:, None] < j[None, :]).astype(np.float32)
    mneg = np.where(j[None, :] <= j[:, None], 0.0, -1e30).astype(np.float32)
    tl = np.arange(512)
    mdiag = np.stack([((128 * i + j[:, None]) < tl[None, :]).astype(np.float32) for i in range(4)], 0)
    fv = np.broadcast_to((2.0 * 0.5 ** (np.arange(NBIS) + 1)).astype(np.float32)[None, :], (128, NBIS)).copy()
    return {"c_ident": ident, "c_ones": ones, "c_tri": -tri, "c_invf": invf_bc, "c_mneg": mneg, "c_mdiag": mdiag, "c_fv": fv}


def phase0(k):
    S = k.S
    I = k.inp
    c = k.c
    es = k.es
    sb = lambda name, shape, dt=F32: k.sb(es, name, shape, dt)
    c["ident"] = sb("ident", [128, 128], BF16)
    c["ones_bf"] = sb("ones_bf", [128, 128], BF16)
    c["zl"] = sb("zl", [128, 128], BF16)
    c["zr"] = sb("zr", [128, 512], BF16)
    S.add("pool", lambda e: e.memset(c["zl"][0][:, :], 0.0), writes=[c["zl"][1]])
    S.add("pool", lambda e: e.memset(c["zr"][0][:, :], 0.0), writes=[c["zr"][1]])
    S.dma("pool", DMA(c["ident"][0][:, :], I["c_ident"]), c["ident"][1], writes=[c["ident"][1]])
    S.dma("pool", DMA(c["ones_bf"][0][:, :], I["c_ones"]), c["ones_bf"][1], writes=[c["ones_bf"][1]])
    sb1 = lambda name, shape, dt=F32: k.sb(k.es1, name, shape, dt)
    for nm, src in (("g1_bc", "g_ffn1"), ("gm_bc", "g_mix")):
        c[nm] = sb1(nm, [128, DM])
        S.dma("sp", DMA(c[nm][0][:, :], I[src].to_broadcast([128, DM])), c[nm][1], writes=[c[nm][1]])
    cos, Bcs = sb1("cos", [128, NTB, 32])
    sin, _ = sb1("sin", [128, NTB, 32])
    c["cos"], c["sin"], c["Bcs"] = cos, sin, Bcs
    with contextlib.ExitStack() as tes:
        posi, Bposi = k.sb(tes, "posi", [128, NTB], I32)
        posf, Bposf = k.sb(tes, "posf", [128, NTB])
        invf, Binvf = k.sb(tes, "invf", [128, 32])
        ang, Bang = k.sb(tes, "ang", [128, NTB, 32])
        ang2, Bang2 = k.sb(tes, "ang2", [128, NTB, 32])
        S.dma("sp", DMA(posi[:, :], I["pos"]), Bposi, writes=[Bposi])
        S.dma("sp", DMA(invf[:, :], I["c_invf"]), Binvf, writes=[Binvf])
        S.add("dve", CP(posf[:, :], posi[:, :]), reads=[Bposi], writes=[Bposf])
        S.add("dve", TT(ang[:, :, :], posf[:, :].unsqueeze(2).to_broadcast([128, NTB, 32]),
                        invf[:, :].unsqueeze(1).to_broadcast([128, NTB, 32]), ALU.mult),
              reads=[Bposf, Binvf], writes=[Bang])
        twopi = float(2.0 * np.pi)
        ki, Bki = k.sb(tes, "ki", [128, NTB, 32], I32)
        kf, Bkf = k.sb(tes, "kf", [128, NTB, 32])
        mm_, Bmm = k.sb(tes, "mm_", [128, NTB, 32])
        S.add("dve", TS(ang2[:, :, :], ang[:, :, :], float(np.pi / 2), None, ALU.add), reads=[Bang], writes=[Bang2])
        for (a, Ba, dst) in ((ang, Bang, sin), (ang2, Bang2, cos)):
            S.add("dve", TS(kf[:, :, :], a[:, :, :], 1.0 / twopi, None, ALU.mult), reads=[Ba], writes=[Bkf])
            S.add("dve", CP(ki[:, :, :], kf[:, :, :]), reads=[Bkf], writes=[Bki])
            S.add("dve", CP(kf[:, :, :], ki[:, :, :]), reads=[Bki], writes=[Bkf])
            S.add("dve", STT(a[:, :, :], kf[:, :, :], -twopi, a[:, :, :], ALU.mult, ALU.add), reads=[Bkf, Ba], writes=[Ba])
            S.add("dve", TS(mm_[:, :, :], a[:, :, :], float(np.pi), twopi, ALU.is_gt, ALU.mult), reads=[Ba], writes=[Bmm])
            S.add("dve", TT(a[:, :, :], a[:, :, :], mm_[:, :, :], ALU.subtract), reads=[Ba, Bmm], writes=[Ba])
            S.add("dve", TS(a[:, :, :], a[:, :, :], float(np.pi), float(-np.pi), ALU.min, ALU.max), reads=[Ba], writes=[Ba])
            S.add("act", ACT(dst[:, :, :], a[:, :, :], AF.Sin), reads=[Ba], writes=[Bcs])
        S.barrier()


SCRATCH = [
    ("x1", [SEQ, DM], F32), ("qaT", [4, 128, SEQ], BF16), ("qiT", [4, 128, SEQ], BF16),
    ("kaT", [128, SEQ], BF16), ("kiT", [128, SEQ], BF16), ("widx", [SEQ, 8], F32),
    ("vh", [SEQ, 8, 65], BF16), ("qbT", [4, 128, SEQ], BF16), ("kbT", [4, 128, SEQ], BF16),
    ("vb", [SEQ, 512], BF16), ("gaT", [DM, SEQ], BF16), ("gbT", [DM, SEQ], BF16),
    ("yaT", [4, 128, SEQ], BF16), ("ybT", [4, 128, SEQ], BF16),
]


def build_nc(debug=False, phases=(1, 2, 3, 4, 5)):
    k = K(debug)
    nc = k.nc
    k.c = {}
    k.chi = 0
    I = {}
    I["x"] = k.din("x", [SEQ, DM])
    I["pos"] = k.din("pos", [128, NTB], I32)
    for nm, shp in WEIGHT_SPECS:
        I[nm] = k.din(nm, shp)
    for nm, arr in host_consts().items():
        I[nm] = k.din(nm, list(arr.shape))
    k.inp = I
    out = nc.dram_tensor("out", [SEQ, DM], F32, kind="ExternalOutput").ap()
    D = {}
    for nm, shp, dt in SCRATCH:
        D[nm] = k.dscratch(nm, shp, dt)
    k.D = D
    with k.es:
        k.alloc_banks()
        k.es1 = contextlib.ExitStack()
        with k.es1:
            phase0(k)
            if 1 in phases:
                phase1(k, D)
        if 2 in phases:
            phase2(k, D)
        if 4 in phases:
            phase4(k, D)
        fw = []
        if 5 in phases:
            fw = phase5(k, D, out)
        k.S.emit(final_waits=fw)
    return nc


def make_in_maps(inputs):
    consts = host_consts()
    x = np.asarray(inputs["x"], dtype=np.float32)
    pos = np.asarray(inputs["positions"], dtype=np.int32)
    shared = {}
    for nm, shp in WEIGHT_SPECS:
        a = np.asarray(inputs[nm], dtype=np.float32)
        shared[nm] = np.ascontiguousarray(a.reshape(shp))
    shared.update(consts)
    maps = []
    for b in range(x.shape[0]):
        m = dict(shared)
        m["x"] = np.ascontiguousarray(x[b])
        m["pos"] = np.ascontiguousarray(pos[b].reshape(NTB, 128).T)
        maps.append(m)
    return maps


def kernel(**inputs):
    maps = make_in_maps(inputs)
    nc = build_nc()
    res = run_bass_kernel_spmd(nc, maps, core_ids=list(range(len(maps))))
    return np.stack([np.asarray(r["out"], dtype=np.float32) for r in res.results], axis=0)


def phase4(k, D):
    S = k.S
    I = k.inp
    with contextlib.ExitStack() as pes:
        sb = lambda name, shape, dt=F32: k.sb(pes, name, shape, dt)
        kbT, BkbT = sb("kbT_s", [128, 4, SEQ], BF16)
        vb, Bvb = sb("vb_s", [128, NTB, 512], BF16)
        tmpc, Btmpc = sb("tmpc", [128, 128])
        trin, Btrin = sb("trin", [128, 128], F32R)
        onen, Bonen = sb("onen", [128, 128], F32R)
        md, Bmd = sb("md", [128, 4, 512])
        mdb, Bmdb = sb("mdb", [128, 4, 512], BF16)
        qb = [sb("qb_s%d" % i, [128, 4, 512], BF16) for i in range(2)]
        NCH = 2
        eb = [sb("e_s%d" % i, [128, 2, 512]) for i in range(2)]
        spb = [sb("sp_s%d" % i, [128, 2, 512], F32R) for i in range(2)]
        Ab = [sb("A_s%d" % i, [128, 2, 512], BF16) for i in range(2)]
        rs_, Brs = sb("rsum", [128, 2, 512], F32R)
        yst = [sb("yst%d" % i, [64, 8, 512], BF16) for i in range(2)]
        for a in range(4):
            S.dma("sp", DMA(kbT[:, a, :], D["kbT"][0][a, :, :]), BkbT, reads=[D["kbT"][1]], accum=[BkbT])
        vb_r = D["vb"][0].rearrange("(b p) f -> p b f", p=128)
        for q in range(0, NTB, 8):
            S.dma("sp", DMA(vb[:, q:q + 8, :], vb_r[:, q:q + 8, :]), Bvb, reads=[D["vb"][1]], accum=[Bvb])
        S.dma("sp", DMA(tmpc[:, :], I["c_tri"]), Btmpc, writes=[Btmpc])
        S.add("dve", CP(trin[:, :], tmpc[:, :]), reads=[Btmpc], writes=[Btrin])
        S.add("pool", lambda e: e.memset(tmpc[:, :], -1.0), reads=[Btmpc], writes=[Btmpc])
        S.add("dve", CP(onen[:, :], tmpc[:, :]), reads=[Btmpc], writes=[Bonen])
        S.dma("sp", DMA(md[:, :, :], I["c_mdiag"].rearrange("i p t -> p i t")), Bmd, writes=[Bmd])
        S.add("dve", CP(mdb[:, :, :], md[:, :, :]), reads=[Bmd], writes=[Bmdb])
        for qc in range(NTT):
            t0 = qc * 512
            q_, Bq = qb[qc % 2]
            ys, Bys = yst[qc % 2]
            S.dma("sp", DMA(q_[:, :, :], D["qbT"][0][:, :, t0:t0 + 512].rearrange("a p t -> p a t")), Bq,
                  reads=[D["qbT"][1]], writes=[Bq])
            nkb = 4 * qc + 4
            for hg in range(4):
                heads = [2 * hg, 2 * hg + 1]

                def zmm(c_, s):
                    h = heads[c_]
                    pr, base = h // 2, (h % 2) * 64
                    kb = nkb - 1 - s
                    zb, Bzb = k.banks[2 * (s % 2) + c_]
                    S.add("pe", MM(zb[:, :], kbT[base:base + 64, pr, kb * 128:(kb + 1) * 128], q_[base:base + 64, pr, :], True, True),
                          reads=[BkbT, Bq], writes=[Bzb])

                def exp1(s):
                    p = s % 2
                    e_, Be = eb[p]
                    S.add("act", ACT(e_[:, :, :], k.pairs[p][:, :, :], AF.Exp), reads=[k.banks[2 * p][1], k.banks[2 * p + 1][1]], writes=[Be])

                for c_ in range(NCH):
                    zmm(c_, 0)
                exp1(0)
                for s in range(nkb):
                    kb = nkb - 1 - s
                    di = kb - 4 * qc
                    p = s % 2
                    e_, Be = eb[p]
                    sp_, Bsp = spb[p]
                    A_, BA = Ab[p]
                    Bz0, Bz1 = k.banks[2 * p][1], k.banks[2 * p + 1][1]
                    if s + 1 < nkb:
                        for c_ in range(NCH):
                            zmm(c_, s + 1)
                    S.add("act", ACT(sp_[:, :, :], e_[:, :, :], AF.Ln, bias=1.0), reads=[Be], writes=[Bsp])
                    if di >= 0:
                        S.add("dve", TT(sp_[:, :, :], sp_[:, :, :].bitcast(F32), md[:, di, :].unsqueeze(1).to_broadcast([128, 2, 512]), ALU.mult),
                              reads=[Bsp, Bmd], writes=[Bsp])
                    for c_ in range(NCH):
                        zb, Bzb = k.banks[2 * p + c_]
                        S.add("pe", MM(zb[:, :], trin[:, :], sp_[:, c_, :], False, s == 0), reads=[Btrin, Bsp], accum=[Bzb])
                        if s > 0:
                            S.add("pe", MM(zb[:, :], onen[:, :], rs_[:, c_, :], False, True), reads=[Bonen, Brs], accum=[Bzb])
                    if s + 1 < nkb:
                        exp1(s + 1)
                        if s > 0:
                            S.add("dve", TT(rs_[:, :, :], rs_[:, :, :].bitcast(F32), sp_[:, :, :].bitcast(F32), ALU.add), reads=[Bsp, Brs], writes=[Brs])
                        else:
                            S.add("dve", CP(rs_[:, :, :], sp_[:, :, :].bitcast(F32)), reads=[Bsp], writes=[Brs])
                    S.add("act", ACT(A_[:, :, :], k.pairs[p][:, :, :], AF.Exp), reads=[Bz0, Bz1], writes=[BA])
                    if di >= 0:
                        S.add("dve", TT(A_[:, :, :], A_[:, :, :], mdb[:, di, :].unsqueeze(1).to_broadcast([128, 2, 512]), ALU.mult),
                              reads=[BA, Bmdb], writes=[BA])
                    for c_ in range(NCH):
                        h = heads[c_]
                        po, Bpo = k.banks[4 + c_]
                        S.add("pe", MM(po[0:64, :], vb[:, kb, h * 64:(h + 1) * 64], A_[:, c_, :], s == 0, s == nkb - 1),
                              reads=[Bvb, BA], writes=[Bpo] if s == 0 else (), accum=() if s == 0 else [Bpo])
                    if s < nkb - 1:
                        k.dummies(DUM4, k.banks[4])
                for c_ in range(NCH):
                    po, Bpo = k.banks[4 + c_]
                    k.copy("dve", ys[:, heads[c_], :], po[0:64, :], [Bpo], [Bys])
            S.dma("sp", DMA(D["ybT"][0].rearrange("a (two d) t -> d (a two) t", two=2)[:, :, t0:t0 + 512], ys[:, :, :]), Bys,
                  reads=[Bys], accum=[D["ybT"][1]])
        S.barrier()


def phase2(k, D):
    S = k.S
    I = k.inp
    c = k.c
    ident, Bid = c["ident"]
    with contextlib.ExitStack() as pes:
        sb = lambda name, shape, dt=F32: k.sb(pes, name, shape, dt)
        kiT, BkiT = sb("kiT_s", [128, SEQ], BF16)
        kaT, BkaT = sb("kaT_s", [128, SEQ], BF16)
        vh, Bvh = sb("vh_s", [128, NTB, 8 * 65], BF16)
        widx, Bwidx = sb("widx_s", [128, NTB, 8])
        mneg, Bmneg = sb("mneg", [128, 128])
        tmpc, Btmpc = sb("tmpc2", [128, 128])
        oner, Boner = sb("oner", [128, 128], F32R)
        fv, Bfv = sb("fv", [128, NBIS])
        qi = [sb("qi_s%d" % i, [128, 4, 512], BF16) for i in range(2)]
        qa, Bqa = sb("qa_s", [128, 4, 512], BF16)
        scb = [sb("sc%d" % i, [128, SEQ]) for i in range(2)]
        tmp = [sb("sctmp%d" % i, [128, 512], F32R) for i in range(3)]
        identf, Bidf = sb("identf", [128, 128])
        dgs = [sb("dg%d" % i, [128, 8, 128], F32R) for i in range(2)]
        selb = [sb("selbf%d" % i, [128, SEQ], BF16) for i in range(2)]
        selTs = [sb("selT%d" % i, [128, NTB, 512], BF16) for i in range(2)]
        st = [sb("bis%d" % i, [128, 8]) for i in range(2)]
        w2f = [sb("w2f%d" % i, [128, NBIS]) for i in range(2)]
        NCH = 2
        Pb = [[sb("P_s%d_%d" % (c_, i), [128, 512], BF16) for i in range(2)] for c_ in range(NCH)]
        rd, Brd = sb("rd", [128, 512], F32R)
        ysb, Bysb = sb("ysb", [64, 512])
        ys, Bys = sb("ysta", [64, 8, 512], BF16)
        S.dma("sp", DMA(kiT[:, :], D["kiT"][0]), BkiT, reads=[D["kiT"][1]], writes=[BkiT])
        S.dma("sp", DMA(kaT[:, :], D["kaT"][0]), BkaT, reads=[D["kaT"][1]], writes=[BkaT])
        vh_r = D["vh"][0].rearrange("(b p) h d -> p b (h d)", p=128)
        for q in range(0, NTB, 8):
            S.dma("sp", DMA(vh[:, q:q + 8, :], vh_r[:, q:q + 8, :]), Bvh, reads=[D["vh"][1]], accum=[Bvh])
        S.dma("sp", DMA(widx[:, :, :], D["widx"][0].rearrange("(b p) h -> p b h", p=128)), Bwidx, reads=[D["widx"][1]], writes=[Bwidx])
        S.dma("sp", DMA(mneg[:, :], I["c_mneg"]), Bmneg, writes=[Bmneg])
        S.dma("sp", DMA(fv[:, :], I["c_fv"]), Bfv, writes=[Bfv])
        S.dma("sp", DMA(identf[:, :], I["c_ident"]), Bidf, writes=[Bidf])
        S.add("pool", lambda e: e.memset(tmpc[:, :], 1.0), writes=[Btmpc])
        S.add("dve", CP(oner[:, :], tmpc[:, :]), reads=[Btmpc], writes=[Boner])
        negb, Bnegb = sb("negb", [128, 1])
        S.add("pool", lambda e: e.memset(negb[:, :], -30000.0), writes=[Bnegb])
        ctr = {"u": 0}

        def ibank():
            ctr["u"] += 1
            return k.banks[6 + (ctr["u"] % 2)]

        def zbank():
            ctr["z"] = ctr.get("z", 0) + 1
            return k.banks[ctr["z"] % 3]

        def index_steps(qc):
            steps = []
            qi_, Bqi = qi[qc % 2]
            selT, BselT = selTs[qc % 2]
            t0 = qc * 512

            def start():
                S.dma("sp", DMA(qi_[:, :, :], D["qiT"][0][:, :, t0:t0 + 512].rearrange("a p t -> p a t")), Bqi, reads=[D["qiT"][1]], writes=[Bqi])

            def start2():
                S.add("pool", lambda e: e.memset(selT[:, 4 * qc:4 * qc + 4, :], -30000.0), writes=[BselT])
            groups = {}
            for j in range(4):
                b = 4 * qc + j
                pre, bis, post = [], [], []
                if j == 0:
                    pre.append(start)
                    post.append(start2)
                groups[j] = (pre, bis, post)
                W = 128 * (b + 1)
                sc, Bsc = scb[b % 2]
                selbf, Bsel = selb[b % 2]
                ts_ = slice(j * 128, (j + 1) * 128)
                dg, Bdg = dgs[b % 2]

                def mkdg(dg=dg, Bdg=Bdg, b=b):
                    S.add("dve", TT(dg[:, :, :], identf[:, :].unsqueeze(1).to_broadcast([128, 8, 128]),
                                    widx[:, b, :].unsqueeze(2).to_broadcast([128, 8, 128]), ALU.mult),
                          reads=[Bidf, Bwidx], writes=[Bdg])
                pre.append(mkdg)
                for si in range(qc + 1):
                    ss_ = slice(si * 512, (si + 1) * 512)
                    for h in range(8):
                        def score(h=h, ss_=ss_, sc=sc, Bsc=Bsc, ts_=ts_, b=b, dg=dg, Bdg=Bdg):
                            pacc, Bpacc = k.banks[3]

                            def dots(hh):
                                pr, base = hh // 2, (hh % 2) * 64
                                ph, Bph = ibank()
                                S.add("pe", MM(ph[:, :], qi_[base:base + 64, pr, ts_], kiT[base:base + 64, ss_], True, True),
                                      reads=[Bqi, BkiT], writes=[Bph])
                                ctr["t"] = ctr.get("t", 0) + 1
                                tm, Btm = tmp[ctr["t"] % 3]
                                S.add("act", ACT(tm[:, :], ph[:, :], AF.Relu), reads=[Bph], writes=[Btm])
                                return tm, Btm
                            if h == 0:
                                ctr["pend"] = dots(0)
                            tm, Btm = ctr["pend"]
                            if h < 7:
                                ctr["pend"] = dots(h + 1)
                            S.add("pe", MM(pacc[:, :], dg[:, h, :], tm[:, :], h == 0, h == 7), reads=[Bdg, Btm],
                                  writes=[Bpacc] if h == 0 else (), accum=() if h == 0 else [Bpacc])
                            if h == 7:
                                S.add("act", ACP(sc[:, ss_], pacc[:, :]), reads=[Bpacc], writes=[Bsc])
                        pre.append(score)

                def maskdiag(sc=sc, Bsc=Bsc, b=b, W=W):
                    S.add("pool", TT(sc[:, b * 128:W], sc[:, b * 128:W], mneg[:, :], ALU.add), reads=[Bsc, Bmneg], writes=[Bsc])
                pre.append(maskdiag)
                if b < 2:
                    def selall(sc=sc, Bsc=Bsc, W=W, selbf=selbf, Bsel=Bsel):
                        S.add("dve", TS(selbf[:, 0:W], sc[:, 0:W], -1e29, None, ALU.is_gt), reads=[Bsc], writes=[Bsel])
                    bis.append(selall)
                else:
                    bs, Bbs = st[b % 2]
                    wf, Bwf = w2f[b % 2]
                    lo, w0, mid, cnt, gpm, hi = (bs[:, i:i + 1] for i in range(6))

                    def binit(sc=sc, Bsc=Bsc, W=W, b=b, bs=bs, Bbs=Bbs, wf=wf, Bwf=Bwf, lo=lo, w0=w0, mid=mid, hi=hi):
                        S.add("dve", RED(hi, sc[:, 0:W], ALU.max), reads=[Bsc], writes=[Bbs])
                        S.add("dve", RED(lo, sc[:, 0:b * 128], ALU.min), reads=[Bsc], writes=[Bbs])
                        S.add("dve", TS(lo, lo, -1.0, None, ALU.add), reads=[Bbs], writes=[Bbs])
                        S.add("dve", TT(w0, hi, lo, ALU.subtract), reads=[Bbs], writes=[Bbs])
                        S.add("dve", TS(wf[:, :], fv[:, :], w0, None, ALU.mult), reads=[Bfv, Bbs], writes=[Bwf])
                        S.add("dve", STT(mid, w0, 0.5, lo, ALU.mult, ALU.add), reads=[Bbs], writes=[Bbs])
                    bis.append(binit)
                    for it in range(NBIS):
                        def bstep(it=it, sc=sc, Bsc=Bsc, W=W, Bbs=Bbs, wf=wf, Bwf=Bwf, mid=mid, cnt=cnt, gpm=gpm, selbf=selbf, Bsel=Bsel):
                            S.add("dve", TS(selbf[:, 0:W], sc[:, 0:W], mid, 0.0, ALU.is_gt, ALU.add, accum_out=cnt), reads=[Bsc, Bbs], writes=[Bsel, Bbs])
                            S.add("dve", TS(gpm, cnt, TOPK - 0.5, 0.5, ALU.is_gt, ALU.subtract), reads=[Bbs], writes=[Bbs])
                            if it + 1 < NBIS:
                                S.add("dve", STT(mid, gpm, wf[:, it + 1:it + 2], mid, ALU.mult, ALU.add), reads=[Bbs, Bwf], writes=[Bbs])
                            else:
                                S.add("dve", TS(gpm, gpm, -0.5, 0.5, ALU.add, ALU.mult), reads=[Bbs], writes=[Bbs])
                                S.add("dve", STT(mid, gpm, wf[:, it:it + 1], mid, ALU.mult, ALU.add), reads=[Bbs, Bwf], writes=[Bbs])
                        bis.append(bstep)

                    def selthr(sc=sc, Bsc=Bsc, W=W, selbf=selbf, Bsel=Bsel, Bbs=Bbs, mid=mid):
                        S.add("dve", TS(selbf[:, 0:W], sc[:, 0:W], mid, None, ALU.is_gt), reads=[Bsc, Bbs], writes=[Bsel])
                    bis.append(selthr)
                for kb0 in range(0, b + 1, 4):
                    def tr(kb0=kb0, b=b, selbf=selbf, Bsel=Bsel, ts_=ts_):
                        n = min(4, b + 1 - kb0)
                        pb, Bpb = ibank()
                        pbf = pb[:].bitcast(BF16)
                        for i in range(n):
                            S.add("pe", TR(pbf[:, i * 128:(i + 1) * 128], selbf[:, (kb0 + i) * 128:(kb0 + i + 1) * 128], ident[:, :]),
                                  reads=[Bsel, Bid], writes=[Bpb] if i == 0 else (), accum=() if i == 0 else [Bpb])
                        S.add("act", ACT(selT[:, kb0:kb0 + n, ts_], pbf[:, 0:n * 128].rearrange("p (a t) -> p a t", a=n), AF.Identity,
                                         scale=30000.0, bias=negb[:, 0:1]), reads=[Bpb, Bnegb], writes=[BselT])
                    post.append(tr)
            return groups

        def attn_steps(qc):
            steps = []
            selT, BselT = selTs[qc % 2]
            t0 = qc * 512
            nkb = 4 * qc + 4

            def start():
                S.dma("sp", DMA(qa[:, :, :], D["qaT"][0][:, :, t0:t0 + 512].rearrange("a p t -> p a t")), Bqa, reads=[D["qaT"][1]], writes=[Bqa])
            steps.append(start)
            for hg in range(4):
                heads = [2 * hg, 2 * hg + 1]

                zbs = {}

                def zmm(c_, s, heads=heads, zbs=zbs):
                    h = heads[c_]
                    pr, base = h // 2, (h % 2) * 64
                    zb, Bzb = zbank()
                    zbs[(c_, s)] = (zb, Bzb)
                    S.add("pe", MM(zb[:, :], kaT[base:base + 64, s * 128:(s + 1) * 128], qa[base:base + 64, pr, :], True, False),
                          reads=[BkaT, Bqa], writes=[Bzb])
                    S.add("pe", MM(zb[:, :], ident[:, :], selT[:, s, :], False, True), reads=[Bid, BselT], accum=[Bzb])

                def pro(zmm=zmm):
                    for c_ in range(NCH):
                        zmm(c_, 0)
                steps.append(pro)
                for s in range(nkb):
                    def step(s=s, zmm=zmm, heads=heads, zbs=zbs):
                        if s + 1 < nkb:
                            zmm(0, s + 1)
                        for c_ in range(NCH):
                            zb, Bzb = zbs.pop((c_, s))
                            P_, BP = Pb[c_][s % 2]
                            S.add("act", ACT(P_[:, :], zb[:, :], AF.Exp, scale=0.125), reads=[Bzb], writes=[BP])
                            if c_ == 0 and s + 1 < nkb:
                                zmm(1, s + 1)
                        for c_ in range(NCH):
                            h = heads[c_]
                            P_, BP = Pb[c_][s % 2]
                            po, Bpo = k.banks[4 + c_]
                            S.add("pe", MM(po[0:65, :], vh[:, s, h * 65:(h + 1) * 65], P_[:, :], s == 0, s == nkb - 1),
                                  reads=[Bvh, BP], writes=[Bpo] if s == 0 else (), accum=() if s == 0 else [Bpo])
                        if s < nkb - 1:
                            k.dummies(DUM2, k.banks[4])
                    steps.append(step)

                def fin(heads=heads):
                    for c_ in range(NCH):
                        h = heads[c_]
                        po, Bpo = k.banks[4 + c_]
                        S.add("dve", lambda e, po=po: e.reciprocal(out=rd[64:65, :], in_=po[64:65, :]), reads=[Bpo], writes=[Brd])
                        S.add("act", ACP(ysb[:, :], po[0:64, :]), reads=[Bpo], writes=[Bysb])
                        pbc, Bpbc = ibank()
                        S.add("pe", MM(pbc[0:64, :], oner[64:65, 0:64], rd[64:65, :], True, True), reads=[Boner, Brd], writes=[Bpbc])
                        S.add("dve", TT(ys[:, h, :], ysb[:, :], pbc[0:64, :], ALU.mult), reads=[Bysb, Bpbc], writes=[Bys])
                steps.append(fin)

            def store():
                S.dma("sp", DMA(D["yaT"][0].rearrange("a (two d) t -> d (a two) t", two=2)[:, :, t0:t0 + 512], ys[:, :, :]), Bys,
                      reads=[Bys], accum=[D["yaT"][1]])
            steps.append(store)
            return steps

        G = {}
        for qc in range(NTT):
            g = index_steps(qc)
            for j in range(4):
                G[4 * qc + j] = g[j]

        def idx_list(qc):
            out = []
            for j in range(4):
                b = 4 * qc + j
                if b + 1 < NTB:
                    out.extend(G[b + 1][0])
                out.extend(G[b][1])
                out.extend(G[b][2])
            return out

        for f in G[0][0]:
            f()
        for f in idx_list(0):
            f()
        for qc in range(NTT):
            sa = attn_steps(qc)
            si_ = idx_list(qc + 1) if qc + 1 < NTT else []
            ia = ii = 0
            na, ni = len(sa), len(si_)
            while ia < na or ii < ni:
                if ia < na and (ii >= ni or ia * ni <= ii * na):
                    sa[ia]()
                    ia += 1
                else:
                    si_[ii]()
                    ii += 1
        S.barrier()


def phase5(k, D, out):
    S = k.S
    I = k.inp
    c = k.c
    with contextlib.ExitStack() as pes:
        sb = lambda name, shape, dt=F32: k.sb(pes, name, shape, dt)
        c["hbf"] = sb("hbf5", [128, 4, DM], BF16)
        c["hT"] = sb("hT5", [128, 8, 512], BF16)
        aT, _ = sb("aT5", [128, NFC, 512], BF16)
        c["aT"] = (aT, [Buf("aT5_%d" % i) for i in range(NFC)])
        c["sg"] = [sb("sg50", [128, 512], BF16), sb("sg51", [128, 512], BF16)]
        c["ss"] = sb("ss5", [128, 4])
        xt, Bxt = sb("xt5", [128, 4, DM])
        wd, Bwd = sb("wd5", [128, NFC, DM], BF16)
        ws = WStream(k, [sb("wb5%d" % i, [128, 8, 512], BF16) for i in range(3)])
        woa, Bwoa = sb("woa", [128, 4, DM], BF16)
        wob, Bwob = sb("wob", [128, 4, DM], BF16)
        wout, Bwout = sb("wout", [128, 8, DM], BF16)
        ya, Bya = sb("ya5", [128, 4, 512], BF16)
        yb, Byb = sb("yb5", [128, 4, 512], BF16)
        gab = [sb("ga5%d" % i, [128, 512], BF16) for i in range(2)]
        gbb = [sb("gb5%d" % i, [128, 512], BF16) for i in range(2)]
        m1 = [sb("m1_%d" % i, [128, 512]) for i in range(2)]
        m2 = [sb("m2_%d" % i, [128, 512]) for i in range(2)]
        mg, Bmg = sb("mg", [128, 8, 512], BF16)
        g2_bc, Bg2 = sb("g2_bc", [128, DM])
        S.dma("sp", DMA(g2_bc[:, :], I["g_ffn2"].to_broadcast([128, DM])), Bg2, writes=[Bg2])
        Wd_r = I["w2_down"].rearrange("(fc p) d -> p fc d", p=128)
        for q in range(0, NFC, 6):
            n = min(6, NFC - q)
            S.dma("pool", DMA(wd[:, q:q + n, :], Wd_r[:, q:q + n, :]), Bwd, accum=[Bwd])
        S.dma("pool", DMA(woa[:, :, :], I["w_o_a"].rearrange("(kc p) f -> p kc f", p=128)), Bwoa, writes=[Bwoa])
        S.dma("pool", DMA(wob[:, :, :], I["w_o_b"].rearrange("(kc p) f -> p kc f", p=128)), Bwob, writes=[Bwob])
        S.dma("pool", DMA(wout[:, :, :], I["w_out"].rearrange("(kc p) f -> p kc f", p=128)), Bwout, writes=[Bwout])
        Wg_r = I["w2_gate"].rearrange("(kc p) f -> p kc f", p=128)
        Wu_r = I["w2_up"].rearrange("(kc p) f -> p kc f", p=128)
        for tt in range(NTT):
            push_ffn_weights(ws, Wg_r, Wu_r)
        x1_r = D["x1"][0].rearrange("(b p) d -> p b d", p=128)
        out_r = out.rearrange("(b p) d -> p b d", p=128)
        for tt in range(NTT):
            t0 = tt * 512
            S.dma("sp", DMA(xt[:, :, :], x1_r[:, tt * 4:(tt + 1) * 4, :]), Bxt, reads=[D["x1"][1]], writes=[Bxt])
            S.dma("sp", DMA(ya[:, :, :], D["yaT"][0][:, :, t0:t0 + 512].rearrange("a p t -> p a t")), Bya, reads=[D["yaT"][1]], writes=[Bya])
            S.dma("sp", DMA(yb[:, :, :], D["ybT"][0][:, :, t0:t0 + 512].rearrange("a p t -> p a t")), Byb, reads=[D["ybT"][1]], writes=[Byb])
            for mc in range(8):
                ga, Bga = gab[mc % 2]
                gb, Bgb = gbb[mc % 2]
                S.dma("sp", DMA(ga[:, :], D["gaT"][0][mc * 128:(mc + 1) * 128, t0:t0 + 512]), Bga, reads=[D["gaT"][1]], writes=[Bga])
                S.dma("sp", DMA(gb[:, :], D["gbT"][0][mc * 128:(mc + 1) * 128, t0:t0 + 512]), Bgb, reads=[D["gbT"][1]], writes=[Bgb])
                pa, Bpa = k.bank()
                pb, Bpb = k.bank()
                for pr in range(4):
                    S.add("pe", MM(pa[:, :], woa[:, pr, mc * 128:(mc + 1) * 128], ya[:, pr, :], pr == 0, pr == 3),
                          reads=[Bwoa, Bya], writes=[Bpa] if pr == 0 else (), accum=() if pr == 0 else [Bpa])
                for pr in range(4):
                    S.add("pe", MM(pb[:, :], wob[:, pr, mc * 128:(mc + 1) * 128], yb[:, pr, :], pr == 0, pr == 3),
                          reads=[Bwob, Byb], writes=[Bpb] if pr == 0 else (), accum=() if pr == 0 else [Bpb])
                a1, Ba1 = m1[mc % 2]
                a2, Ba2 = m2[mc % 2]
                S.add("dve", TT(a1[:, :], ga[:, :], pa[:, :], ALU.mult), reads=[Bga, Bpa], writes=[Ba1])
                S.add("dve", TT(a2[:, :], gb[:, :], pb[:, :], ALU.mult), reads=[Bgb, Bpb], writes=[Ba2])
                S.add("pool", TT(mg[:, mc, :], a1[:, :], a2[:, :], ALU.add), reads=[Ba1, Ba2], writes=[Bmg])
            for j in range(4):
                for half in range(2):
                    po, Bpo = k.bank()
                    for mc in range(8):
                        S.add("pe", MM(po[:, :], mg[:, mc, j * 128:(j + 1) * 128], wout[:, mc, half * 512:(half + 1) * 512], mc == 0, mc == 7),
                              reads=[Bmg, Bwout], writes=[Bpo] if mc == 0 else (), accum=() if mc == 0 else [Bpo])
                    sl = xt[:, j, half * 512:(half + 1) * 512]
                    S.add("dve", TT(sl, sl, po[:, :], ALU.add), reads=[Bpo, Bxt], writes=[Bxt])
            emit_ffn(k, c, xt, Bxt, g2_bc, Bg2, ws, wd, Bwd)
            S.dma("sp", DMA(out_r[:, tt * 4:(tt + 1) * 4, :], xt[:, :, :]), Bxt, reads=[Bxt])
        return [Bxt]
```

```python
import numpy as np
import concourse.bass as bass
import concourse.mybir as mybir
from concourse.bass_utils import run_bass_kernel_spmd

F32 = mybir.dt.float32
F32R = mybir.dt.float32r
BF16 = mybir.dt.bfloat16
I32 = mybir.dt.int32
AF = mybir.ActivationFunctionType
ALU = mybir.AluOpType
AX = mybir.AxisListType

ENGS = ("pe", "act", "dve", "pool", "sp")


class Buf:
    __slots__ = ("name", "writers", "readers", "sem", "cnt")

    def __init__(self, name):
        self.name = name
        self.writers = []
        self.readers = []
        self.sem = None
        self.cnt = 0


class Op:
    __slots__ = ("eng", "fn", "deps", "dma", "slot", "tok_val", "needed", "seq", "idx")

    def __init__(self, eng, fn):
        self.eng = eng
        self.fn = fn
        self.deps = []
        self.dma = False
        self.slot = None
        self.tok_val = 0
        self.needed = False
        self.seq = 0
        self.idx = 0


class Sched:
    def __init__(self, nc):
        self.nc = nc
        self.ops = {e: [] for e in ENGS}
        self.n = 0
        self.slots = []
        self.dmas_since = []
        self.pending = {}

    def _track(self, op, reads, writes, accum=()):
        deps = []
        for b in reads:
            deps.extend(b.writers)
        for b in writes:
            deps.extend(b.writers)
            deps.extend(b.readers)
        for b in accum:
            deps.extend(b.readers)
        seen = set()
        last = {}
        for d in deps:
            if d is op or id(d) in seen:
                continue
            if d.eng == "pe" and op.eng == "pe":
                continue
            seen.add(id(d))
            if (not d.dma) and d.eng in ("pe", "act", "dve"):
                if d.eng not in last or last[d.eng].idx < d.idx:
                    last[d.eng] = d
                continue
            op.deps.append(d)
            d.needed = True
        for d in last.values():
            op.deps.append(d)
            d.needed = True
        for b in reads:
            b.readers.append(op)
        for b in writes:
            b.writers = [op]
            b.readers = []
        for b in accum:
            b.writers.append(op)
            b.readers = []

    def barrier(self):
        deps = [self.ops[e][-1] for e in ENGS if self.ops[e]] + list(self.dmas_since)
        for e in ENGS:
            self.pending[e] = list(self.pending.get(e, [])) + deps
        self.dmas_since = []

    def _apply_pending(self, op):
        p = self.pending.pop(op.eng, None)
        if p:
            seen = set(id(d) for d in op.deps)
            for d in p:
                if d is op or id(d) in seen:
                    continue
                if d.eng == "pe" and op.eng == "pe" and not d.dma:
                    continue
                seen.add(id(d))
                op.deps.append(d)
                d.needed = True

    def begin_capture(self):
        self.cap = []

    def end_capture(self):
        c, self.cap = self.cap, None
        return c

    def replay(self, item):
        if item[0] == "add":
            self.add(*item[1:])
        else:
            self.dma(*item[1:])

    def add(self, eng, fn, reads=(), writes=(), accum=()):
        if getattr(self, "cap", None) is not None:
            self.cap.append(("add", eng, fn, tuple(reads), tuple(writes), tuple(accum)))
            return None
        op = Op(eng, fn)
        op.idx = self.n
        self.n += 1
        self._track(op, reads, writes, accum)
        self._apply_pending(op)
        self.ops[eng].append(op)
        return op

    def dma(self, eng, fn, slot, reads=(), writes=(), accum=()):
        if getattr(self, "cap", None) is not None:
            self.cap.append(("dma", eng, fn, slot, tuple(reads), tuple(writes), tuple(accum)))
            return None
        op = Op(eng, fn)
        op.idx = self.n
        self.n += 1
        op.dma = True
        op.slot = slot
        if slot.sem is None:
            slot.sem = True
            self.slots.append(slot)
        slot.cnt += 16
        op.tok_val = slot.cnt
        self._track(op, reads, writes, accum)
        self._apply_pending(op)
        self.dmas_since.append(op)
        self.ops[eng].append(op)
        return op

    def emit(self, final_waits=()):
        nc = self.nc
        import contextlib
        with contextlib.ExitStack() as es:
            es.enter_context(nc.allow_low_precision("bf16/fp32r matmul operands by design"))
            esem = {e: es.enter_context(nc.semaphore("s_" + e)) for e in ENGS}
            for s in self.slots:
                s.sem = es.enter_context(nc.semaphore("d_" + s.name))
            for e in ENGS:
                k = 0
                for op in self.ops[e]:
                    if not op.dma and op.needed:
                        k += 1
                        op.seq = k
            block = es.enter_context(nc.Block())

            def tok(op):
                if op.dma:
                    return op.slot.sem, op.tok_val
                return esem[op.eng], op.seq

            def run(e, engine):
                waited = {}
                for op in self.ops[e]:
                    need = {}
                    for d in op.deps:
                        s, v = tok(d)
                        key = s.name
                        if waited.get(key, 0) >= v:
                            continue
                        if key not in need or need[key][1] < v:
                            need[key] = (s, v)
                    for key, (s, v) in need.items():
                        waited[key] = v
                        engine.wait_ge(s, v)
                    ins = op.fn(engine)
                    if op.dma:
                        ins.then_inc(op.slot.sem, 16)
                    elif op.needed:
                        ins.then_inc(esem[e], 1)
                if e == "sp":
                    for b in final_waits:
                        engine.wait_ge(b.sem, b.cnt)

            @block.tensor
            def _(eng):
                run("pe", eng)

            @block.scalar
            def _(eng):
                run("act", eng)

            @block.vector
            def _(eng):
                run("dve", eng)

            @block.gpsimd
            def _(eng):
                run("pool", eng)

            @block.sync
            def _(eng):
                run("sp", eng)


def MM(out, lhsT, rhs, start, stop):
    return lambda e: e.matmul(out, lhsT=lhsT, rhs=rhs, start=start, stop=stop)


def TR(out, in_, ident):
    return lambda e: e.transpose(out, in_, ident)


def ACT(out, in_, func, **kw):
    return lambda e: e.activation(out=out, in_=in_, func=func, **kw)


def TT(out, in0, in1, op):
    return lambda e: e.tensor_tensor(out=out, in0=in0, in1=in1, op=op)


def TS(out, in0, s1, s2, op0, op1=None, **kw):
    if op1 is None:
        return lambda e: e.tensor_scalar(out=out, in0=in0, scalar1=s1, scalar2=s2, op0=op0, **kw)
    return lambda e: e.tensor_scalar(out=out, in0=in0, scalar1=s1, scalar2=s2, op0=op0, op1=op1, **kw)


def STT(out, in0, scalar, in1, op0, op1):
    return lambda e: e.scalar_tensor_tensor(out=out, in0=in0, scalar=scalar, in1=in1, op0=op0, op1=op1)


def CP(out, in_):
    return lambda e: e.tensor_copy(out=out, in_=in_)


def ACP(out, in_):
    return lambda e: e.copy(out=out, in_=in_)


def DMA(out, in_, slow=False):
    if slow:
        return lambda e: e.dma_start(out=out, in_=in_, allow_slow_non_contiguous=True)
    return lambda e: e.dma_start(out=out, in_=in_)


def RED(out, in_, op, **kw):
    return lambda e: e.tensor_reduce(out=out, in_=in_, axis=AX.X, op=op, **kw)


SEQ = 4096
DM = 1024
DFF = 2816
NFC = DFF // 128
NTT = SEQ // 512
NTB = SEQ // 128
INC = 4104
EPS = 1e-6
TOPK = 256
NBIS = 18
DUM4 = 0
DUM2 = 0


import contextlib


class WStream:
    def __init__(self, k, bufs):
        self.k = k
        self.bufs = bufs
        self.items = []
        self.i = 0
        self.j = 0

    def push(self, src3):
        self.items.append(src3)

    def _issue(self):
        src = self.items[self.j]
        ap, B = self.bufs[self.j % len(self.bufs)]
        kc, w = src.shape[1], src.shape[2]
        self.k.S.dma("pool", DMA(ap[:, 0:kc, 0:w], src), B, writes=[B])
        self.j += 1

    def get(self):
        n = len(self.bufs)
        while self.j < len(self.items) and self.j < self.i + n - 1:
            self._issue()
        ap, B = self.bufs[self.i % n]
        self.i += 1
        return ap, B


class K:
    def __init__(self, debug=False):
        self.debug = debug
        self.nc = bass.Bass("TRN2", target_bir_lowering=False)
        self.S = Sched(self.nc)
        self.es = contextlib.ExitStack()
        self.bi = 0
        self.rr = 0
        self.outs = {}

    def din(self, name, shape, dt=F32):
        return self.nc.dram_tensor(name, list(shape), dt, kind="ExternalInput").ap()

    def dscratch(self, name, shape, dt):
        kind = "ExternalOutput" if self.debug else "Internal"
        ap = self.nc.dram_tensor(name, list(shape), dt, kind=kind).ap()
        return ap, Buf(name)

    def sb(self, es, name, shape, dt=F32):
        t = es.enter_context(self.nc.sbuf_tensor(name, list(shape), dt))
        return t, Buf(name)

    def alloc_banks(self):
        self.banks = []
        for i in range(8):
            t = self.es.enter_context(self.nc.psum_tensor("ps%d" % i, [128, 512], F32))
            self.banks.append((t, Buf("ps%d" % i)))

    def bank(self):
        pool = getattr(self, "pool", None) or list(range(8))
        self.bi += 1
        return self.banks[pool[self.bi % len(pool)]]

    def dummies(self, n, bank):
        zl, Bzl = self.c["zl"]
        zr, Bzr = self.c["zr"]
        b, Bb = bank
        for _ in range(n):
            self.S.add("pe", MM(b[:, :], zl[:, :], zr[:, :], False, False), reads=[Bzl, Bzr], accum=[Bb])

    def ev(self):
        self.rr ^= 1
        return "act" if self.rr else "dve"

    def copy(self, eng, out, in_, reads, writes):
        if eng == "act":
            self.S.add("act", ACP(out, in_), reads=reads, writes=writes)
        else:
            self.S.add(eng, CP(out, in_), reads=reads, writes=writes)


def emit_rstd(k, rs, Brs, scale):
    S = k.S
    S.add("dve", TS(rs, rs, scale, EPS, ALU.mult, ALU.add), reads=[Brs], writes=[Brs])
    S.add("act", ACT(rs, rs, AF.Sqrt), reads=[Brs], writes=[Brs])
    S.add("dve", lambda e: e.reciprocal(out=rs, in_=rs), reads=[Brs], writes=[Brs])


def emit_norm_T(k, c, xt, Bxt, g_bc, Bg):
    S = k.S
    hbf, Bhbf = c["hbf"]
    hT, BhT = c["hT"]
    ss, Bss = c["ss"]
    ident, Bid = c["ident"]
    S.add("pool", lambda e: e.memset(ss[:, 0:4], 0.0), writes=[Bss])
    for j in range(4):
        S.add("act", ACT(hbf[:, j, :], xt[:, j, :], AF.Square, accum_out=ss[:, j:j + 1]),
              reads=[Bxt], writes=[Bhbf, Bss])
    emit_rstd(k, ss[:, 0:4], Bss, 1.0 / DM)
    for j in range(4):
        S.add("dve", STT(hbf[:, j, :], xt[:, j, :], ss[:, j:j + 1], g_bc[:, :], ALU.mult, ALU.mult),
              reads=[Bxt, Bss, Bg], writes=[Bhbf])
    for kc in range(8):
        pb, Bpb = k.bank()
        pbf = pb[:].bitcast(BF16)
        for j in range(4):
            S.add("pe", TR(pbf[:, j * 128:(j + 1) * 128], hbf[:, j, kc * 128:(kc + 1) * 128], ident[:, :]),
                  reads=[Bhbf, Bid], writes=[Bpb] if j == 0 else (), accum=() if j == 0 else [Bpb])
        k.copy(k.ev(), hT[:, kc, :], pbf[:, 0:512], [Bpb], [BhT[kc]])


def emit_ffn(k, c, xt, Bxt, g_bc, Bg, ws, wd, Bwd):
    S = k.S
    emit_norm_T(k, c, xt, Bxt, g_bc, Bg)
    hT, BhT = c["hT"]
    aT, BaT = c["aT"]
    groups = [(0, 4), (4, 4), (8, 4), (12, 4), (16, 4), (20, 2)]
    for (f0, nf) in groups:
        wg, Bwg = ws.get()
        wu, Bwu = ws.get()
        for fi in range(nf):
            fc = f0 + fi
            pg, Bpg = k.bank()
            pu, Bpu = k.bank()
            for kc in range(8):
                S.add("pe", MM(pg[:, :], wg[:, kc, fi * 128:(fi + 1) * 128], hT[:, kc, :], kc == 0, kc == 7),
                      reads=[Bwg, BhT[kc]], writes=[Bpg] if kc == 0 else (), accum=() if kc == 0 else [Bpg])
            for kc in range(8):
                S.add("pe", MM(pu[:, :], wu[:, kc, fi * 128:(fi + 1) * 128], hT[:, kc, :], kc == 0, kc == 7),
                      reads=[Bwu, BhT[kc]], writes=[Bpu] if kc == 0 else (), accum=() if kc == 0 else [Bpu])
            sg, Bsg = c["sg"][fc % 2]
            S.add("act", ACT(sg[:, :], pg[:, :], AF.Silu), reads=[Bpg], writes=[Bsg])
            S.add("dve", TT(aT[:, fc, :], sg[:, :], pu[:, :], ALU.mult), reads=[Bsg, Bpu], writes=[BaT[fc]])
    for j in range(4):
        for half in range(2):
            po, Bpo = k.bank()
            for fc in range(NFC):
                S.add("pe", MM(po[:, :], aT[:, fc, j * 128:(j + 1) * 128], wd[:, fc, half * 512:(half + 1) * 512],
                               fc == 0, fc == NFC - 1),
                      reads=[BaT[fc], Bwd], writes=[Bpo] if fc == 0 else (), accum=() if fc == 0 else [Bpo])
            sl = xt[:, j, half * 512:(half + 1) * 512]
            S.add("dve", STT(sl, po[:, :], 0.5, sl, ALU.mult, ALU.add), reads=[Bpo, Bxt], writes=[Bxt])


def push_ffn_weights(ws, Wg_r, Wu_r):
    for (f0, nf) in [(0, 4), (4, 4), (8, 4), (12, 4), (16, 4), (20, 2)]:
        ws.push(Wg_r[:, :, f0 * 128:(f0 + nf) * 128])
        ws.push(Wu_r[:, :, f0 * 128:(f0 + nf) * 128])


def emit_rope(k, eng, out1, out2, x1, x2, cosb, sinb, t, Bt, reads, writes):
    S = k.S
    t1, t2, t3, t4 = t
    S.add(eng, TT(t1, x1, cosb, ALU.mult), reads=reads, writes=[Bt])
    S.add(eng, TT(t2, x2, sinb, ALU.mult), reads=reads + [Bt], writes=[Bt])
    S.add(eng, TT(t3, x1, sinb, ALU.mult), reads=reads + [Bt], writes=[Bt])
    S.add(eng, TT(t4, x2, cosb, ALU.mult), reads=reads + [Bt], writes=[Bt])
    S.add(eng, TT(out1, t1, t2, ALU.subtract), reads=[Bt], writes=writes)
    S.add(eng, TT(out2, t3, t4, ALU.add), reads=[Bt], writes=writes)


def phase1(k, D):
    nc, S = k.nc, k.S
    I = k.inp
    with contextlib.ExitStack() as pes:
        sb = lambda name, shape, dt=F32: k.sb(pes, name, shape, dt)
        c = k.c
        c["hbf"] = sb("hbf", [128, 4, DM], BF16)
        c["hT"] = (sb("hT", [128, 8, 512], BF16)[0], [Buf("hT_%d" % i) for i in range(8)])
        aT, _ = sb("aT", [128, NFC, 512], BF16)
        c["aT"] = (aT, [Buf("aT%d" % i) for i in range(NFC)])
        c["sg"] = [sb("sg0", [128, 512], BF16), sb("sg1", [128, 512], BF16)]
        c["ss"] = sb("ss", [128, 4])
        xt, Bxt = sb("xt", [128, 4, DM])
        wd, Bwd = sb("wd", [128, NFC, DM], BF16)
        ws = WStream(k, [sb("wb%d" % i, [128, 8, 520], BF16) for i in range(4)])
        w_uq, Bwuq = sb("w_uq", [128, 2, 512], BF16)
        w_qi, Bwqi = sb("w_qi", [128, 2, 512], BF16)
        w_uv, Bwuv = sb("w_uv", [128, 512], BF16)
        gq_bc, Bgq = sb("gq_bc", [128, 512])
        gk_bc, Bgk = sb("gk_bc", [128, 64])
        gcq, Bgcq = sb("gcq", [128, 2])
        cq32, Bcq32 = sb("cq32", [128, 2, 512])
        sqb, Bsqb = sb("sqb", [128, 2, 512], BF16)
        rbc, Brbc = sb("rbc", [128, 512])
        cqn, Bcqn = sb("cqn", [128, 2, 512], BF16)
        qn, Bqn = sb("qn", [128, 512])
        qsq, Bqsq = sb("qsq", [128, 512])
        r8, Br8 = sb("r8", [128, 8])
        tq, Btq = sb("tq", [128, 4, 256])
        qr, Bqr = sb("qr", [128, 512], BF16)
        qi_r, Bqir = sb("qi_r", [128, 512], BF16)
        tq2, Btq2 = sb("tq2", [128, 4, 256])
        qaT_st, BqaT = sb("qaT_st", [128, 4, 512], BF16)
        qiT_st, BqiT = sb("qiT_st", [128, 4, 512], BF16)
        kk, Bkk = sb("kk", [128, 2, 128], BF16)
        kn, Bkn = sb("kn", [128, 64])
        tk, Btk = sb("tk", [128, 4, 32])
        r1, Br1 = sb("r1", [128, 1])
        kaT_st, BkaT = sb("kaT_st", [128, 512], BF16)
        kiT_st, BkiT = sb("kiT_st", [128, 512], BF16)
        wi_st, Bwi = sb("wi_st", [128, 4, 8])
        vaT, BvaT = sb("vaT", [128, 512], BF16)
        vh_st, Bvh = sb("vh_st", [128, 4, 8, 65], BF16)
        ch_st = [sb("ch_st%d" % i, [128, 512], BF16) for i in range(3)]
        vb_st, Bvb = sb("vb_st", [128, 4, 512], BF16)
        ident, Bid = c["ident"]
        ones_bf, Bones = c["ones_bf"]
        cos, sin, Bcs = c["cos"], c["sin"], c["Bcs"]
        g1_bc, Bg1 = c["g1_bc"]
        gm_bc, Bgm = c["gm_bc"]

        Wd_r = I["w1_down"].rearrange("(fc p) d -> p fc d", p=128)
        for q in range(0, NFC, 6):
            n = min(6, NFC - q)
            S.dma("pool", DMA(wd[:, q:q + n, :], Wd_r[:, q:q + n, :]), Bwd, accum=[Bwd])
        S.dma("pool", DMA(w_uq[:, :, :], I["w_uq_a"].rearrange("(kc p) f -> p kc f", p=128)), Bwuq, writes=[Bwuq])
        S.dma("pool", DMA(w_qi[:, :, :], I["w_q_idx"].rearrange("(kc p) f -> p kc f", p=128)), Bwqi, writes=[Bwqi])
        S.dma("pool", DMA(w_uv[:, :].rearrange("c (h d) -> c h d", h=8), I["w_uv_a"].rearrange("h c d -> c h d")), Bwuv, writes=[Bwuv])
        S.dma("sp", DMA(gq_bc[:, :].rearrange("p (h d) -> p h d", h=8),
                        I["g_q_a"].rearrange("(o h) d -> o h d", o=1).to_broadcast([128, 8, 64])), Bgq, writes=[Bgq])
        S.dma("sp", DMA(gk_bc[:, :], I["g_k_a"].to_broadcast([128, 64])), Bgk, writes=[Bgk])
        S.dma("sp", DMA(gcq[:, :], I["g_cq"].rearrange("o (kc p) -> p (o kc)", p=128), slow=True), Bgcq, writes=[Bgcq])
        S.add("pool", lambda e: e.memset(vh_st[:, :, :, 64:65], 1.0), writes=[Bvh])

        Wg_r = I["w1_gate"].rearrange("(kc p) f -> p kc f", p=128)
        Wu_r = I["w1_up"].rearrange("(kc p) f -> p kc f", p=128)
        Win_r = I["w_in"].rearrange("(kc p) f -> p kc f", p=128)
        pieces = [(0, 520), (520, 512), (1032, 512), (1544, 512), (2056, 512), (2568, 512), (3080, 512), (3592, 512)]
        for tt in range(NTT):
            push_ffn_weights(ws, Wg_r, Wu_r)
            for (c0, w) in pieces:
                ws.push(Win_r[:, :, c0:c0 + w])

        x_r = I["x"].rearrange("(b p) d -> p b d", p=128)
        for tt in range(NTT):
            t0 = tt * 512
            S.dma("sp", DMA(xt[:, :, :], x_r[:, tt * 4:(tt + 1) * 4, :]), Bxt, writes=[Bxt])
            emit_ffn(k, c, xt, Bxt, g1_bc, Bg1, ws, wd, Bwd)
            S.dma("sp", DMA(D["x1"][0].rearrange("(b p) d -> p b d", p=128)[:, tt * 4:(tt + 1) * 4, :], xt[:, :, :]),
                  Bxt, reads=[Bxt], accum=[D["x1"][1]])
            emit_norm_T(k, c, xt, Bxt, gm_bc, Bgm)
            hT, BhT = c["hT"]
            S.begin_capture()
            k.pool = None
            wA, BwA = ws.get()
            pcs = []
            for ci in range(2):
                pc, Bpc = k.bank()
                for kc in range(8):
                    S.add("pe", MM(pc[:, :], wA[:, kc, ci * 128:(ci + 1) * 128], hT[:, kc, :], kc == 0, kc == 7),
                          reads=[BwA, BhT[kc]], writes=[Bpc] if kc == 0 else (), accum=() if kc == 0 else [Bpc])
                S.add("act", ACP(cq32[:, ci, :], pc[:, :]), reads=[Bpc], writes=[Bcq32])
                S.add("act", ACT(sqb[:, ci, :], pc[:, :], AF.Square), reads=[Bpc], writes=[Bsqb])
            pv, Bpv = k.bank()
            for kc in range(8):
                S.add("pe", MM(pv[:, :], wA[:, kc, 320:448], hT[:, kc, :], kc == 0, kc == 7),
                      reads=[BwA, BhT[kc]], writes=[Bpv] if kc == 0 else (), accum=() if kc == 0 else [Bpv])
            k.copy("dve", vaT[:, :], pv[:, :], [Bpv], [BvaT])
            pss, Bpss = k.bank()
            for ci in range(2):
                S.add("pe", MM(pss[:, :], ones_bf[:, :], sqb[:, ci, :], ci == 0, ci == 1),
                      reads=[Bones, Bsqb], writes=[Bpss] if ci == 0 else (), accum=() if ci == 0 else [Bpss])
            S.add("dve", TS(rbc[:, :], pss[:, :], 1.0 / 256, EPS, ALU.mult, ALU.add), reads=[Bpss], writes=[Brbc])
            S.add("act", ACT(rbc[:, :], rbc[:, :], AF.Sqrt), reads=[Brbc], writes=[Brbc])
            S.add("dve", lambda e: e.reciprocal(out=rbc[:, :], in_=rbc[:, :]), reads=[Brbc], writes=[Brbc])
            for ci in range(2):
                S.add("dve", STT(cqn[:, ci, :], cq32[:, ci, :], gcq[:, ci:ci + 1], rbc[:, :], ALU.mult, ALU.mult),
                      reads=[Bcq32, Bgcq, Brbc], writes=[Bcqn])
            for j in range(4):
                b = tt * 4 + j
                ts_ = slice(j * 128, (j + 1) * 128)
                cosb = cos[:, b, :].unsqueeze(1).to_broadcast([128, 8, 32])
                sinb = sin[:, b, :].unsqueeze(1).to_broadcast([128, 8, 32])
                pqa, Bpqa = k.bank()
                for ci in range(2):
                    S.add("pe", MM(pqa[:, :], cqn[:, ci, ts_], w_uq[:, ci, :], ci == 0, ci == 1),
                          reads=[Bcqn, Bwuq], writes=[Bpqa] if ci == 0 else (), accum=() if ci == 0 else [Bpqa])
                pqi, Bpqi = k.bank()
                for ci in range(2):
                    S.add("pe", MM(pqi[:, :], cqn[:, ci, ts_], w_qi[:, ci, :], ci == 0, ci == 1),
                          reads=[Bcqn, Bwqi], writes=[Bpqi] if ci == 0 else (), accum=() if ci == 0 else [Bpqi])
                S.add("act", ACT(qsq[:, :], pqa[:, :], AF.Square), reads=[Bpqa], writes=[Bqsq])
                S.add("dve", RED(r8[:, :], qsq[:, :].rearrange("p (h d) -> p h d", h=8), ALU.add), reads=[Bqsq], writes=[Br8])
                emit_rstd(k, r8[:, :], Br8, 1.0 / 64)
                S.add("dve", TT(qn[:, :].rearrange("p (h d) -> p h d", h=8), pqa[:, :].rearrange("p (h d) -> p h d", h=8),
                                r8[:, :].unsqueeze(2).to_broadcast([128, 8, 64]), ALU.mult),
                      reads=[Bpqa, Br8], writes=[Bqn])
                S.add("dve", TT(qn[:, :], qn[:, :], gq_bc[:, :], ALU.mult), reads=[Bqn, Bgq], writes=[Bqn])
                qn3 = qn[:, :].rearrange("p (h d) -> p h d", h=8)
                qr3 = qr[:, :].rearrange("p (h d) -> p h d", h=8)
                tqs = [tq[:, i, :].rearrange("p (h d) -> p h d", h=8) for i in range(4)]
                emit_rope(k, "dve", qr3[:, :, 0:32], qr3[:, :, 32:64], qn3[:, :, 0:32], qn3[:, :, 32:64], cosb, sinb,
                          tqs, Btq, [Bqn, Bcs], [Bqr])
                S.add("act", ACP(qsq[:, :], pqi[:, :]), reads=[Bpqi], writes=[Bqsq])
                qs3 = qsq[:, :].rearrange("p (h d) -> p h d", h=8)
                qi3 = qi_r[:, :].rearrange("p (h d) -> p h d", h=8)
                tq2s = [tq2[:, i, :].rearrange("p (h d) -> p h d", h=8) for i in range(4)]
                emit_rope(k, "pool", qi3[:, :, 0:32], qi3[:, :, 32:64], qs3[:, :, 0:32], qs3[:, :, 32:64], cosb, sinb,
                          tq2s, Btq2, [Bqsq, Bcs], [Bqir])
                for (src, Bsrc, dst, Bdst) in ((qr, Bqr, qaT_st, BqaT), (qi_r, Bqir, qiT_st, BqiT)):
                    pb, Bpb = k.bank()
                    pbf = pb[:].bitcast(BF16)
                    for pr in range(4):
                        S.add("pe", TR(pbf[:, pr * 128:(pr + 1) * 128], src[:, pr * 128:(pr + 1) * 128], ident[:, :]),
                              reads=[Bsrc, Bid], writes=[Bpb] if pr == 0 else (), accum=() if pr == 0 else [Bpb])
                    k.copy(k.ev(), dst[:, :, ts_], pbf[:, 0:512].rearrange("p (a t) -> p a t", a=4), [Bpb], [Bdst])
                pk, Bpk = k.bank()
                for kc in range(8):
                    S.add("pe", MM(pk[:, 0:264], hT[:, kc, ts_], wA[:, kc, 256:520], kc == 0, kc == 7),
                          reads=[BwA, BhT[kc]], writes=[Bpk] if kc == 0 else (), accum=() if kc == 0 else [Bpk])
                S.add("pool", lambda e: e.memset(r1[:, :], 0.0), writes=[Br1])
                S.add("act", ACT(kn[:, :], pk[:, 0:64], AF.Square, accum_out=r1[:, 0:1]), reads=[Bpk], writes=[Bkn, Br1])
                emit_rstd(k, r1[:, :], Br1, 1.0 / 64)
                S.add("dve", STT(kn[:, :], pk[:, 0:64], r1[:, 0:1], gk_bc[:, :], ALU.mult, ALU.mult),
                      reads=[Bpk, Br1, Bgk], writes=[Bkn])
                tks = [tk[:, i, :] for i in range(4)]
                emit_rope(k, "dve", kk[:, 0, 0:32], kk[:, 0, 32:64], kn[:, 0:32], kn[:, 32:64], cos[:, b, :], sin[:, b, :],
                          tks, Btk, [Bkn, Bcs], [Bkk])
                emit_rope(k, "dve", kk[:, 1, 0:32], kk[:, 1, 32:64], pk[:, 192:224], pk[:, 224:256], cos[:, b, :], sin[:, b, :],
                          tks, Btk, [Bpk, Bcs], [Bkk])
                S.add("dve", CP(kk[:, :, 64:128], kk[:, :, 0:64]), reads=[Bkk], writes=[Bkk])
                S.add("dve", TS(wi_st[:, j, :], pk[:, 256:264], 1.0 / (8.0 ** 0.5 * 8.0), None, ALU.mult), reads=[Bpk], writes=[Bwi])
                pb, Bpb = k.bank()
                pbf = pb[:].bitcast(BF16)
                for i2 in range(2):
                    S.add("pe", TR(pbf[:, i2 * 128:(i2 + 1) * 128], kk[:, i2, :], ident[:, :]),
                          reads=[Bkk, Bid], writes=[Bpb] if i2 == 0 else (), accum=() if i2 == 0 else [Bpb])
                k.copy("act", kaT_st[:, ts_], pbf[:, 0:128], [Bpb], [BkaT])
                k.copy("act", kiT_st[:, ts_], pbf[:, 128:256], [Bpb], [BkiT])
                pvh, Bpvh = k.bank()
                S.add("pe", MM(pvh[:, :], vaT[:, ts_], w_uv[:, :], True, True), reads=[BvaT, Bwuv], writes=[Bpvh])
                k.copy("act", vh_st[:, j, :, 0:64], pvh[:, :].rearrange("p (h d) -> p h d", h=8), [Bpvh], [Bvh])
            S.dma("sp", DMA(D["qaT"][0][:, :, t0:t0 + 512].rearrange("a p t -> p a t"), qaT_st[:, :, :]), BqaT, reads=[BqaT], accum=[D["qaT"][1]])
            S.dma("sp", DMA(D["qiT"][0][:, :, t0:t0 + 512].rearrange("a p t -> p a t"), qiT_st[:, :, :]), BqiT, reads=[BqiT], accum=[D["qiT"][1]])
            S.dma("sp", DMA(D["kaT"][0][:, t0:t0 + 512], kaT_st[:, :]), BkaT, reads=[BkaT], accum=[D["kaT"][1]])
            S.dma("sp", DMA(D["kiT"][0][:, t0:t0 + 512], kiT_st[:, :]), BkiT, reads=[BkiT], accum=[D["kiT"][1]])
            S.dma("sp", DMA(D["widx"][0].rearrange("(b p) h -> p b h", p=128)[:, tt * 4:(tt + 1) * 4, :], wi_st[:, :, :]), Bwi, reads=[Bwi], accum=[D["widx"][1]])
            S.dma("sp", DMA(D["vh"][0].rearrange("(b p) h d -> p b h d", p=128)[:, tt * 4:(tt + 1) * 4, :, :], vh_st[:, :, :, :]), Bvh, reads=[Bvh], accum=[D["vh"][1]])
            X = S.end_capture()
            S.begin_capture()
            k.pool = None
            for (nm, scl) in (("qbT", 0.125), ("kbT", 1.0)):
                wB, BwB = ws.get()
                for pr in range(4):
                    pq, Bpq = k.bank()
                    for kc in range(8):
                        S.add("pe", MM(pq[:, :], wB[:, kc, pr * 128:(pr + 1) * 128], hT[:, kc, :], kc == 0, kc == 7),
                              reads=[BwB, BhT[kc]], writes=[Bpq] if kc == 0 else (), accum=() if kc == 0 else [Bpq])
                    st, Bst = ch_st[k.chi % 3]
                    k.chi += 1
                    S.add("act", ACT(st[:, :], pq[:, :], AF.Copy, scale=scl), reads=[Bpq], writes=[Bst])
                    S.dma("sp", DMA(D[nm][0][pr, :, t0:t0 + 512], st[:, :]), Bst, reads=[Bst], accum=[D[nm][1]])
            wDp, BwD = ws.get()
            for j in range(4):
                pq, Bpq = k.bank()
                for kc in range(8):
                    S.add("pe", MM(pq[:, :], hT[:, kc, j * 128:(j + 1) * 128], wDp[:, kc, 0:512], kc == 0, kc == 7),
                          reads=[BwD, BhT[kc]], writes=[Bpq] if kc == 0 else (), accum=() if kc == 0 else [Bpq])
                k.copy(k.ev(), vb_st[:, j, :], pq[:, :], [Bpq], [Bvb])
            S.dma("sp", DMA(D["vb"][0].rearrange("(b p) f -> p b f", p=128)[:, tt * 4:(tt + 1) * 4, :], vb_st[:, :, :]), Bvb, reads=[Bvb], accum=[D["vb"][1]])
            for gi, nm in enumerate(("gaT", "gaT", "gbT", "gbT")):
                wG, BwG = ws.get()
                for mc in range(4):
                    pq, Bpq = k.bank()
                    for kc in range(8):
                        S.add("pe", MM(pq[:, :], wG[:, kc, mc * 128:(mc + 1) * 128], hT[:, kc, :], kc == 0, kc == 7),
                              reads=[BwG, BhT[kc]], writes=[Bpq] if kc == 0 else (), accum=() if kc == 0 else [Bpq])
                    st, Bst = ch_st[k.chi % 3]
                    k.chi += 1
                    S.add("act", ACT(st[:, :], pq[:, :], AF.Sigmoid), reads=[Bpq], writes=[Bst])
                    m0 = ((gi % 2) * 4 + mc) * 128
                    S.dma("sp", DMA(D[nm][0][m0:m0 + 128, t0:t0 + 512], st[:, :]), Bst, reads=[Bst], accum=[D[nm][1]])
            Y = S.end_capture()
            k.pool = None
            ix = iy = 0
            nx, ny = len(X), len(Y)
            for it_ in X:
                S.replay(it_)
            for it_ in Y:
                S.replay(it_)
        S.barrier()


WEIGHT_SPECS = [
    ("g_ffn1", [1, DM]), ("w1_gate", [DM, DFF]), ("w1_up", [DM, DFF]), ("w1_down", [DFF, DM]),
    ("g_mix", [1, DM]), ("w_in", [DM, INC]), ("g_cq", [1, 256]), ("w_uq_a", [256, 512]),
    ("w_q_idx", [256, 512]), ("g_q_a", [1, 64]), ("g_k_a", [1, 64]), ("w_uv_a", [8, 128, 64]),
    ("w_o_a", [512, DM]), ("w_o_b", [512, DM]), ("w_out", [DM, DM]), ("g_ffn2", [1, DM]),
    ("w2_gate", [DM, DFF]), ("w2_up", [DM, DFF]), ("w2_down", [DFF, DM]),
]


def host_consts():
    ident = np.eye(128, dtype=np.float32)
    ones = np.ones((128, 128), dtype=np.float32)
    j = np.arange(128)
    tri = (j[:, None] >= j[None, :]).astype(np.float32)
    invf = (10000.0 ** (-np.arange(0, 64, 2, dtype=np.float32) / 64)).astype(np.float32)
    invf_bc = np.broadcast_to(invf[None, :], (128, 32)).copy()
    mstrict = (j[:, None] < j[None, :]).astype(np.float32)
    mneg = np.where(j[None, :] <= j[:, None], 0.0, -1e30).astype(np.float32)
    tl = np.arange(512)
    mdiag = np.stack([((128 * i + j[:, None]) < tl[None, :]).astype(np.float32) for i in range(4)], 0)
    fv = np.broadcast_to((2.0 * 0.5 ** (np.arange(NBIS) + 1)).astype(np.float32)[None, :], (128, NBIS)).copy()
    return {"c_ident": ident, "c_ones": ones, "c_tri": -tri, "c_invf": invf_bc, "c_mneg": mneg, "c_mdiag": mdiag, "c_fv": fv}


def phase0(k):
    S = k.S
    I = k.inp
    c = k.c
    es = k.es
    sb = lambda name, shape, dt=F32: k.sb(es, name, shape, dt)
    c["ident"] = sb("ident", [128, 128], BF16)
    c["ones_bf"] = sb("ones_bf", [128, 128], BF16)
    c["zl"] = sb("zl", [128, 128], BF16)
    c["zr"] = sb("zr", [128, 512], BF16)
    S.add("pool", lambda e: e.memset(c["zl"][0][:, :], 0.0), writes=[c["zl"][1]])
    S.add("pool", lambda e: e.memset(c["zr"][0][:, :], 0.0), writes=[c["zr"][1]])
    S.dma("pool", DMA(c["ident"][0][:, :], I["c_ident"]), c["ident"][1], writes=[c["ident"][1]])
    S.dma("pool", DMA(c["ones_bf"][0][:, :], I["c_ones"]), c["ones_bf"][1], writes=[c["ones_bf"][1]])
    sb1 = lambda name, shape, dt=F32: k.sb(k.es1, name, shape, dt)
    for nm, src in (("g1_bc", "g_ffn1"), ("gm_bc", "g_mix")):
        c[nm] = sb1(nm, [128, DM])
        S.dma("sp", DMA(c[nm][0][:, :], I[src].to_broadcast([128, DM])), c[nm][1], writes=[c[nm][1]])
    cos, Bcs = sb1("cos", [128, NTB, 32])
    sin, _ = sb1("sin", [128, NTB, 32])
    c["cos"], c["sin"], c["Bcs"] = cos, sin, Bcs
    with contextlib.ExitStack() as tes:
        posi, Bposi = k.sb(tes, "posi", [128, NTB], I32)
        posf, Bposf = k.sb(tes, "posf", [128, NTB])
        invf, Binvf = k.sb(tes, "invf", [128, 32])
        ang, Bang = k.sb(tes, "ang", [128, NTB, 32])
        ang2, Bang2 = k.sb(tes, "ang2", [128, NTB, 32])
        S.dma("sp", DMA(posi[:, :], I["pos"]), Bposi, writes=[Bposi])
        S.dma("sp", DMA(invf[:, :], I["c_invf"]), Binvf, writes=[Binvf])
        S.add("dve", CP(posf[:, :], posi[:, :]), reads=[Bposi], writes=[Bposf])
        S.add("dve", TT(ang[:, :, :], posf[:, :].unsqueeze(2).to_broadcast([128, NTB, 32]),
                        invf[:, :].unsqueeze(1).to_broadcast([128, NTB, 32]), ALU.mult),
              reads=[Bposf, Binvf], writes=[Bang])
        twopi = float(2.0 * np.pi)
        ki, Bki = k.sb(tes, "ki", [128, NTB, 32], I32)
        kf, Bkf = k.sb(tes, "kf", [128, NTB, 32])
        mm_, Bmm = k.sb(tes, "mm_", [128, NTB, 32])
        S.add("dve", TS(ang2[:, :, :], ang[:, :, :], float(np.pi / 2), None, ALU.add), reads=[Bang], writes=[Bang2])
        for (a, Ba, dst) in ((ang, Bang, sin), (ang2, Bang2, cos)):
            S.add("dve", TS(kf[:, :, :], a[:, :, :], 1.0 / twopi, None, ALU.mult), reads=[Ba], writes=[Bkf])
            S.add("dve", CP(ki[:, :, :], kf[:, :, :]), reads=[Bkf], writes=[Bki])
            S.add("dve", CP(kf[:, :, :], ki[:, :, :]), reads=[Bki], writes=[Bkf])
            S.add("dve", STT(a[:, :, :], kf[:, :, :], -twopi, a[:, :, :], ALU.mult, ALU.add), reads=[Bkf, Ba], writes=[Ba])
            S.add("dve", TS(mm_[:, :, :], a[:, :, :], float(np.pi), twopi, ALU.is_gt, ALU.mult), reads=[Ba], writes=[Bmm])
            S.add("dve", TT(a[:, :, :], a[:, :, :], mm_[:, :, :], ALU.subtract), reads=[Ba, Bmm], writes=[Ba])
            S.add("dve", TS(a[:, :, :], a[:, :, :], float(np.pi), float(-np.pi), ALU.min, ALU.max), reads=[Ba], writes=[Ba])
            S.add("act", ACT(dst[:, :, :], a[:, :, :], AF.Sin), reads=[Ba], writes=[Bcs])
        S.barrier()


SCRATCH = [
    ("x1", [SEQ, DM], F32), ("qaT", [4, 128, SEQ], BF16), ("qiT", [4, 128, SEQ], BF16),
    ("kaT", [128, SEQ], BF16), ("kiT", [128, SEQ], BF16), ("widx", [SEQ, 8], F32),
    ("vh", [SEQ, 8, 65], BF16), ("qbT", [4, 128, SEQ], BF16), ("kbT", [4, 128, SEQ], BF16),
    ("vb", [SEQ, 512], BF16), ("gaT", [DM, SEQ], BF16), ("gbT", [DM, SEQ], BF16),
    ("yaT", [4, 128, SEQ], BF16), ("ybT", [4, 128, SEQ], BF16),
]


def build_nc(debug=False, phases=(1, 2, 3, 4, 5)):
    k = K(debug)
    nc = k.nc
    k.c = {}
    k.chi = 0
    I = {}
    I["x"] = k.din("x", [SEQ, DM])
    I["pos"] = k.din("pos", [128, NTB], I32)
    for nm, shp in WEIGHT_SPECS:
        I[nm] = k.din(nm, shp)
    for nm, arr in host_consts().items():
        I[nm] = k.din(nm, list(arr.shape))
    k.inp = I
    out = nc.dram_tensor("out", [SEQ, DM], F32, kind="ExternalOutput").ap()
    D = {}
    for nm, shp, dt in SCRATCH:
        D[nm] = k.dscratch(nm, shp, dt)
    k.D = D
    with k.es:
        k.alloc_banks()
        k.es1 = contextlib.ExitStack()
        with k.es1:
            phase0(k)
            if 1 in phases:
                phase1(k, D)
        if 2 in phases:
            phase2(k, D)
        if 4 in phases:
            phase4(k, D)
        fw = []
        if 5 in phases:
            fw = phase5(k, D, out)
        k.S.emit(final_waits=fw)
    return nc


def make_in_maps(inputs):
    consts = host_consts()
    x = np.asarray(inputs["x"], dtype=np.float32)
    pos = np.asarray(inputs["positions"], dtype=np.int32)
    shared = {}
    for nm, shp in WEIGHT_SPECS:
        a = np.asarray(inputs[nm], dtype=np.float32)
        shared[nm] = np.ascontiguousarray(a.reshape(shp))
    shared.update(consts)
    maps = []
    for b in range(x.shape[0]):
        m = dict(shared)
        m["x"] = np.ascontiguousarray(x[b])
        m["pos"] = np.ascontiguousarray(pos[b].reshape(NTB, 128).T)
        maps.append(m)
    return maps


def kernel(**inputs):
    maps = make_in_maps(inputs)
    nc = build_nc()
    res = run_bass_kernel_spmd(nc, maps, core_ids=list(range(len(maps))))
    return np.stack([np.asarray(r["out"], dtype=np.float32) for r in res.results], axis=0)


def phase4(k, D):
    S = k.S
    I = k.inp
    with contextlib.ExitStack() as pes:
        sb = lambda name, shape, dt=F32: k.sb(pes, name, shape, dt)
        kbT, BkbT = sb("kbT_s", [128, 4, SEQ], BF16)
        vb, Bvb = sb("vb_s", [128, NTB, 512], BF16)
        tmpc, Btmpc = sb("tmpc", [128, 128])
        trin, Btrin = sb("trin", [128, 128], F32R)
        onen, Bonen = sb("onen", [128, 128], F32R)
        md, Bmd = sb("md", [128, 4, 512])
        mdb, Bmdb = sb("mdb", [128, 4, 512], BF16)
        qb = [sb("qb_s%d" % i, [128, 4, 512], BF16) for i in range(2)]
        NCH = 2
        eb = [[sb("e_s%d_%d" % (c_, i), [128, 512]) for i in range(2)] for c_ in range(NCH)]
        spb = [[sb("sp_s%d_%d" % (c_, i), [128, 512], F32R) for i in range(2)] for c_ in range(NCH)]
        Ab = [[sb("A_s%d_%d" % (c_, i), [128, 512], BF16) for i in range(2)] for c_ in range(NCH)]
        rs = [sb("rsum%d" % c_, [128, 512], F32R) for c_ in range(NCH)]
        yst = [sb("yst%d" % i, [64, 8, 512], BF16) for i in range(2)]
        for a in range(4):
            S.dma("sp", DMA(kbT[:, a, :], D["kbT"][0][a, :, :]), BkbT, reads=[D["kbT"][1]], accum=[BkbT])
        vb_r = D["vb"][0].rearrange("(b p) f -> p b f", p=128)
        for q in range(0, NTB, 8):
            S.dma("sp", DMA(vb[:, q:q + 8, :], vb_r[:, q:q + 8, :]), Bvb, reads=[D["vb"][1]], accum=[Bvb])
        S.dma("sp", DMA(tmpc[:, :], I["c_tri"]), Btmpc, writes=[Btmpc])
        S.add("dve", CP(trin[:, :], tmpc[:, :]), reads=[Btmpc], writes=[Btrin])
        S.add("pool", lambda e: e.memset(tmpc[:, :], -1.0), reads=[Btmpc], writes=[Btmpc])
        S.add("dve", CP(onen[:, :], tmpc[:, :]), reads=[Btmpc], writes=[Bonen])
        S.dma("sp", DMA(md[:, :, :], I["c_mdiag"].rearrange("i p t -> p i t")), Bmd, writes=[Bmd])
        S.add("dve", CP(mdb[:, :, :], md[:, :, :]), reads=[Bmd], writes=[Bmdb])
        for qc in range(NTT):
            t0 = qc * 512
            q_, Bq = qb[qc % 2]
            ys, Bys = yst[qc % 2]
            S.dma("sp", DMA(q_[:, :, :], D["qbT"][0][:, :, t0:t0 + 512].rearrange("a p t -> p a t")), Bq,
                  reads=[D["qbT"][1]], writes=[Bq])
            nkb = 4 * qc + 4
            for hg in range(4):
                heads = [2 * hg, 2 * hg + 1]

                def zmm(c_, s):
                    h = heads[c_]
                    pr, base = h // 2, (h % 2) * 64
                    kb = nkb - 1 - s
                    zb, Bzb = k.banks[2 * c_ + (s % 2)]
                    S.add("pe", MM(zb[:, :], kbT[base:base + 64, pr, kb * 128:(kb + 1) * 128], q_[base:base + 64, pr, :], True, True),
                          reads=[BkbT, Bq], writes=[Bzb])

                def exp1(c_, s):
                    zb, Bzb = k.banks[2 * c_ + (s % 2)]
                    e_, Be = eb[c_][s % 2]
                    S.add("act", ACT(e_[:, :], zb[:, :], AF.Exp), reads=[Bzb], writes=[Be])

                for c_ in range(NCH):
                    zmm(c_, 0)
                for c_ in range(NCH):
                    exp1(c_, 0)
                for s in range(nkb):
                    kb = nkb - 1 - s
                    di = kb - 4 * qc
                    if s + 1 < nkb:
                        for c_ in range(NCH):
                            zmm(c_, s + 1)
                    for c_ in range(NCH):
                        e_, Be = eb[c_][s % 2]
                        sp_, Bsp = spb[c_][s % 2]
                        S.add("act", ACT(sp_[:, :], e_[:, :], AF.Ln, bias=1.0), reads=[Be], writes=[Bsp])
                        if di >= 0:
                            S.add("dve", TT(sp_[:, :], sp_[:, :].bitcast(F32), md[:, di, :], ALU.mult), reads=[Bsp, Bmd], writes=[Bsp])
                    for c_ in range(NCH):
                        zb, Bzb = k.banks[2 * c_ + (s % 2)]
                        sp_, Bsp = spb[c_][s % 2]
                        r_, Br = rs[c_]
                        S.add("pe", MM(zb[:, :], trin[:, :], sp_[:, :], False, s == 0), reads=[Btrin, Bsp], accum=[Bzb])
                        if s > 0:
                            S.add("pe", MM(zb[:, :], onen[:, :], r_[:, :], False, True), reads=[Bonen, Br], accum=[Bzb])
                    if s + 1 < nkb:
                        for c_ in range(NCH):
                            exp1(c_, s + 1)
                    if s + 1 < nkb:
                        for c_ in range(NCH):
                            sp_, Bsp = spb[c_][s % 2]
                            r_, Br = rs[c_]
                            if s > 0:
                                S.add("dve", TT(r_[:, :], r_[:, :].bitcast(F32), sp_[:, :].bitcast(F32), ALU.add), reads=[Bsp, Br], writes=[Br])
                            else:
                                S.add("dve", CP(r_[:, :], sp_[:, :].bitcast(F32)), reads=[Bsp], writes=[Br])
                    for c_ in range(NCH):
                        zb, Bzb = k.banks[2 * c_ + (s % 2)]
                        A_, BA = Ab[c_][s % 2]
                        S.add("act", ACT(A_[:, :], zb[:, :], AF.Exp), reads=[Bzb], writes=[BA])
                        if di >= 0:
                            S.add("dve", TT(A_[:, :], A_[:, :], mdb[:, di, :], ALU.mult), reads=[BA, Bmdb], writes=[BA])
                    for c_ in range(NCH):
                        h = heads[c_]
                        A_, BA = Ab[c_][s % 2]
                        po, Bpo = k.banks[4 + c_]
                        S.add("pe", MM(po[0:64, :], vb[:, kb, h * 64:(h + 1) * 64], A_[:, :], s == 0, s == nkb - 1),
                              reads=[Bvb, BA], writes=[Bpo] if s == 0 else (), accum=() if s == 0 else [Bpo])
                    if s < nkb - 1:
                        k.dummies(DUM4, k.banks[4])
                for c_ in range(NCH):
                    po, Bpo = k.banks[4 + c_]
                    k.copy("dve", ys[:, heads[c_], :], po[0:64, :], [Bpo], [Bys])
            S.dma("sp", DMA(D["ybT"][0].rearrange("a (two d) t -> d (a two) t", two=2)[:, :, t0:t0 + 512], ys[:, :, :]), Bys,
                  reads=[Bys], accum=[D["ybT"][1]])
        S.barrier()


def phase2(k, D):
    S = k.S
    I = k.inp
    c = k.c
    ident, Bid = c["ident"]
    with contextlib.ExitStack() as pes:
        sb = lambda name, shape, dt=F32: k.sb(pes, name, shape, dt)
        kiT, BkiT = sb("kiT_s", [128, SEQ], BF16)
        kaT, BkaT = sb("kaT_s", [128, SEQ], BF16)
        vh, Bvh = sb("vh_s", [128, NTB, 8 * 65], BF16)
        widx, Bwidx = sb("widx_s", [128, NTB, 8])
        mneg, Bmneg = sb("mneg", [128, 128])
        tmpc, Btmpc = sb("tmpc2", [128, 128])
        oner, Boner = sb("oner", [128, 128], F32R)
        fv, Bfv = sb("fv", [128, NBIS])
        qi = [sb("qi_s%d" % i, [128, 4, 512], BF16) for i in range(2)]
        qa, Bqa = sb("qa_s", [128, 4, 512], BF16)
        scb = [sb("sc%d" % i, [128, SEQ]) for i in range(2)]
        tmp = [sb("sctmp%d" % i, [128, 512], F32R) for i in range(3)]
        identf, Bidf = sb("identf", [128, 128])
        dgs = [sb("dg%d" % i, [128, 8, 128], F32R) for i in range(2)]
        selb = [sb("selbf%d" % i, [128, SEQ], BF16) for i in range(2)]
        selTs = [sb("selT%d" % i, [128, NTB, 512], BF16) for i in range(2)]
        st = [sb("bis%d" % i, [128, 8]) for i in range(2)]
        w2f = [sb("w2f%d" % i, [128, NBIS]) for i in range(2)]
        NCH = 2
        Pb = [[sb("P_s%d_%d" % (c_, i), [128, 512], BF16) for i in range(2)] for c_ in range(NCH)]
        rd, Brd = sb("rd", [128, 512], F32R)
        ysb, Bysb = sb("ysb", [64, 512])
        ys, Bys = sb("ysta", [64, 8, 512], BF16)
        S.dma("sp", DMA(kiT[:, :], D["kiT"][0]), BkiT, reads=[D["kiT"][1]], writes=[BkiT])
        S.dma("sp", DMA(kaT[:, :], D["kaT"][0]), BkaT, reads=[D["kaT"][1]], writes=[BkaT])
        vh_r = D["vh"][0].rearrange("(b p) h d -> p b (h d)", p=128)
        for q in range(0, NTB, 8):
            S.dma("sp", DMA(vh[:, q:q + 8, :], vh_r[:, q:q + 8, :]), Bvh, reads=[D["vh"][1]], accum=[Bvh])
        S.dma("sp", DMA(widx[:, :, :], D["widx"][0].rearrange("(b p) h -> p b h", p=128)), Bwidx, reads=[D["widx"][1]], writes=[Bwidx])
        S.dma("sp", DMA(mneg[:, :], I["c_mneg"]), Bmneg, writes=[Bmneg])
        S.dma("sp", DMA(fv[:, :], I["c_fv"]), Bfv, writes=[Bfv])
        S.dma("sp", DMA(identf[:, :], I["c_ident"]), Bidf, writes=[Bidf])
        S.add("pool", lambda e: e.memset(tmpc[:, :], 1.0), writes=[Btmpc])
        S.add("dve", CP(oner[:, :], tmpc[:, :]), reads=[Btmpc], writes=[Boner])
        negb, Bnegb = sb("negb", [128, 1])
        S.add("pool", lambda e: e.memset(negb[:, :], -30000.0), writes=[Bnegb])
        ctr = {"u": 0}

        def ibank():
            ctr["u"] += 1
            return k.banks[6 + (ctr["u"] % 2)]

        def zbank():
            ctr["z"] = ctr.get("z", 0) + 1
            return k.banks[ctr["z"] % 3]

        def index_steps(qc):
            steps = []
            qi_, Bqi = qi[qc % 2]
            selT, BselT = selTs[qc % 2]
            t0 = qc * 512

            def start():
                S.dma("sp", DMA(qi_[:, :, :], D["qiT"][0][:, :, t0:t0 + 512].rearrange("a p t -> p a t")), Bqi, reads=[D["qiT"][1]], writes=[Bqi])

            def start2():
                S.add("pool", lambda e: e.memset(selT[:, 4 * qc:4 * qc + 4, :], -30000.0), writes=[BselT])
            groups = {}
            for j in range(4):
                b = 4 * qc + j
                pre, bis, post = [], [], []
                if j == 0:
                    pre.append(start)
                    post.append(start2)
                groups[j] = (pre, bis, post)
                W = 128 * (b + 1)
                sc, Bsc = scb[b % 2]
                selbf, Bsel = selb[b % 2]
                ts_ = slice(j * 128, (j + 1) * 128)
                dg, Bdg = dgs[b % 2]

                def mkdg(dg=dg, Bdg=Bdg, b=b):
                    S.add("dve", TT(dg[:, :, :], identf[:, :].unsqueeze(1).to_broadcast([128, 8, 128]),
                                    widx[:, b, :].unsqueeze(2).to_broadcast([128, 8, 128]), ALU.mult),
                          reads=[Bidf, Bwidx], writes=[Bdg])
                pre.append(mkdg)
                for si in range(qc + 1):
                    ss_ = slice(si * 512, (si + 1) * 512)
                    for h in range(8):
                        def score(h=h, ss_=ss_, sc=sc, Bsc=Bsc, ts_=ts_, b=b, dg=dg, Bdg=Bdg):
                            pacc, Bpacc = k.banks[3]

                            def dots(hh):
                                pr, base = hh // 2, (hh % 2) * 64
                                ph, Bph = ibank()
                                S.add("pe", MM(ph[:, :], qi_[base:base + 64, pr, ts_], kiT[base:base + 64, ss_], True, True),
                                      reads=[Bqi, BkiT], writes=[Bph])
                                ctr["t"] = ctr.get("t", 0) + 1
                                tm, Btm = tmp[ctr["t"] % 3]
                                S.add("act", ACT(tm[:, :], ph[:, :], AF.Relu), reads=[Bph], writes=[Btm])
                                return tm, Btm
                            if h == 0:
                                ctr["pend"] = dots(0)
                            tm, Btm = ctr["pend"]
                            if h < 7:
                                ctr["pend"] = dots(h + 1)
                            S.add("pe", MM(pacc[:, :], dg[:, h, :], tm[:, :], h == 0, h == 7), reads=[Bdg, Btm],
                                  writes=[Bpacc] if h == 0 else (), accum=() if h == 0 else [Bpacc])
                            if h == 7:
                                S.add("act", ACP(sc[:, ss_], pacc[:, :]), reads=[Bpacc], writes=[Bsc])
                        pre.append(score)

                def maskdiag(sc=sc, Bsc=Bsc, b=b, W=W):
                    S.add("pool", TT(sc[:, b * 128:W], sc[:, b * 128:W], mneg[:, :], ALU.add), reads=[Bsc, Bmneg], writes=[Bsc])
                pre.append(maskdiag)
                if b < 2:
                    def selall(sc=sc, Bsc=Bsc, W=W, selbf=selbf, Bsel=Bsel):
                        S.add("dve", TS(selbf[:, 0:W], sc[:, 0:W], -1e29, None, ALU.is_gt), reads=[Bsc], writes=[Bsel])
                    bis.append(selall)
                else:
                    bs, Bbs = st[b % 2]
                    wf, Bwf = w2f[b % 2]
                    lo, w0, mid, cnt, gpm, hi = (bs[:, i:i + 1] for i in range(6))

                    def binit(sc=sc, Bsc=Bsc, W=W, b=b, bs=bs, Bbs=Bbs, wf=wf, Bwf=Bwf, lo=lo, w0=w0, mid=mid, hi=hi):
                        S.add("dve", RED(hi, sc[:, 0:W], ALU.max), reads=[Bsc], writes=[Bbs])
                        S.add("dve", RED(lo, sc[:, 0:b * 128], ALU.min), reads=[Bsc], writes=[Bbs])
                        S.add("dve", TS(lo, lo, -1.0, None, ALU.add), reads=[Bbs], writes=[Bbs])
                        S.add("dve", TT(w0, hi, lo, ALU.subtract), reads=[Bbs], writes=[Bbs])
                        S.add("dve", TS(wf[:, :], fv[:, :], w0, None, ALU.mult), reads=[Bfv, Bbs], writes=[Bwf])
                        S.add("dve", STT(mid, w0, 0.5, lo, ALU.mult, ALU.add), reads=[Bbs], writes=[Bbs])
                    bis.append(binit)
                    for it in range(NBIS):
                        def bstep(it=it, sc=sc, Bsc=Bsc, W=W, Bbs=Bbs, wf=wf, Bwf=Bwf, mid=mid, cnt=cnt, gpm=gpm, selbf=selbf, Bsel=Bsel):
                            S.add("dve", TS(selbf[:, 0:W], sc[:, 0:W], mid, 0.0, ALU.is_gt, ALU.add, accum_out=cnt), reads=[Bsc, Bbs], writes=[Bsel, Bbs])
                            S.add("dve", TS(gpm, cnt, TOPK - 0.5, 0.5, ALU.is_gt, ALU.subtract), reads=[Bbs], writes=[Bbs])
                            if it + 1 < NBIS:
                                S.add("dve", STT(mid, gpm, wf[:, it + 1:it + 2], mid, ALU.mult, ALU.add), reads=[Bbs, Bwf], writes=[Bbs])
                            else:
                                S.add("dve", TS(gpm, gpm, -0.5, 0.5, ALU.add, ALU.mult), reads=[Bbs], writes=[Bbs])
                                S.add("dve", STT(mid, gpm, wf[:, it:it + 1], mid, ALU.mult, ALU.add), reads=[Bbs, Bwf], writes=[Bbs])
                        bis.append(bstep)

                    def selthr(sc=sc, Bsc=Bsc, W=W, selbf=selbf, Bsel=Bsel, Bbs=Bbs, mid=mid):
                        S.add("dve", TS(selbf[:, 0:W], sc[:, 0:W], mid, None, ALU.is_gt), reads=[Bsc, Bbs], writes=[Bsel])
                    bis.append(selthr)
                for kb0 in range(0, b + 1, 4):
                    def tr(kb0=kb0, b=b, selbf=selbf, Bsel=Bsel, ts_=ts_):
                        n = min(4, b + 1 - kb0)
                        pb, Bpb = ibank()
                        pbf = pb[:].bitcast(BF16)
                        for i in range(n):
                            S.add("pe", TR(pbf[:, i * 128:(i + 1) * 128], selbf[:, (kb0 + i) * 128:(kb0 + i + 1) * 128], ident[:, :]),
                                  reads=[Bsel, Bid], writes=[Bpb] if i == 0 else (), accum=() if i == 0 else [Bpb])
                        S.add("act", ACT(selT[:, kb0:kb0 + n, ts_], pbf[:, 0:n * 128].rearrange("p (a t) -> p a t", a=n), AF.Identity,
                                         scale=30000.0, bias=negb[:, 0:1]), reads=[Bpb, Bnegb], writes=[BselT])
                    post.append(tr)
            return groups

        def attn_steps(qc):
            steps = []
            selT, BselT = selTs[qc % 2]
            t0 = qc * 512
            nkb = 4 * qc + 4

            def start():
                S.dma("sp", DMA(qa[:, :, :], D["qaT"][0][:, :, t0:t0 + 512].rearrange("a p t -> p a t")), Bqa, reads=[D["qaT"][1]], writes=[Bqa])
            steps.append(start)
            for hg in range(4):
                heads = [2 * hg, 2 * hg + 1]

                zbs = {}

                def zmm(c_, s, heads=heads, zbs=zbs):
                    h = heads[c_]
                    pr, base = h // 2, (h % 2) * 64
                    zb, Bzb = zbank()
                    zbs[(c_, s)] = (zb, Bzb)
                    S.add("pe", MM(zb[:, :], kaT[base:base + 64, s * 128:(s + 1) * 128], qa[base:base + 64, pr, :], True, False),
                          reads=[BkaT, Bqa], writes=[Bzb])
                    S.add("pe", MM(zb[:, :], ident[:, :], selT[:, s, :], False, True), reads=[Bid, BselT], accum=[Bzb])

                def pro(zmm=zmm):
                    for c_ in range(NCH):
                        zmm(c_, 0)
                steps.append(pro)
                for s in range(nkb):
                    def step(s=s, zmm=zmm, heads=heads, zbs=zbs):
                        if s + 1 < nkb:
                            zmm(0, s + 1)
                        for c_ in range(NCH):
                            zb, Bzb = zbs.pop((c_, s))
                            P_, BP = Pb[c_][s % 2]
                            S.add("act", ACT(P_[:, :], zb[:, :], AF.Exp, scale=0.125), reads=[Bzb], writes=[BP])
                            if c_ == 0 and s + 1 < nkb:
                                zmm(1, s + 1)
                        for c_ in range(NCH):
                            h = heads[c_]
                            P_, BP = Pb[c_][s % 2]
                            po, Bpo = k.banks[4 + c_]
                            S.add("pe", MM(po[0:65, :], vh[:, s, h * 65:(h + 1) * 65], P_[:, :], s == 0, s == nkb - 1),
                                  reads=[Bvh, BP], writes=[Bpo] if s == 0 else (), accum=() if s == 0 else [Bpo])
                        if s < nkb - 1:
                            k.dummies(DUM2, k.banks[4])
                    steps.append(step)

                def fin(heads=heads):
                    for c_ in range(NCH):
                        h = heads[c_]
                        po, Bpo = k.banks[4 + c_]
                        S.add("dve", lambda e, po=po: e.reciprocal(out=rd[64:65, :], in_=po[64:65, :]), reads=[Bpo], writes=[Brd])
                        S.add("act", ACP(ysb[:, :], po[0:64, :]), reads=[Bpo], writes=[Bysb])
                        pbc, Bpbc = ibank()
                        S.add("pe", MM(pbc[0:64, :], oner[64:65, 0:64], rd[64:65, :], True, True), reads=[Boner, Brd], writes=[Bpbc])
                        S.add("dve", TT(ys[:, h, :], ysb[:, :], pbc[0:64, :], ALU.mult), reads=[Bysb, Bpbc], writes=[Bys])
                steps.append(fin)

            def store():
                S.dma("sp", DMA(D["yaT"][0].rearrange("a (two d) t -> d (a two) t", two=2)[:, :, t0:t0 + 512], ys[:, :, :]), Bys,
                      reads=[Bys], accum=[D["yaT"][1]])
            steps.append(store)
            return steps

        G = {}
        for qc in range(NTT):
            g = index_steps(qc)
            for j in range(4):
                G[4 * qc + j] = g[j]

        def idx_list(qc):
            out = []
            for j in range(4):
                b = 4 * qc + j
                if b + 1 < NTB:
                    out.extend(G[b + 1][0])
                out.extend(G[b][1])
                out.extend(G[b][2])
            return out

        for f in G[0][0]:
            f()
        for f in idx_list(0):
            f()
        for qc in range(NTT):
            sa = attn_steps(qc)
            si_ = idx_list(qc + 1) if qc + 1 < NTT else []
            ia = ii = 0
            na, ni = len(sa), len(si_)
            while ia < na or ii < ni:
                if ia < na and (ii >= ni or ia * ni <= ii * na):
                    sa[ia]()
                    ia += 1
                else:
                    si_[ii]()
                    ii += 1
        S.barrier()


def phase5(k, D, out):
    S = k.S
    I = k.inp
    c = k.c
    with contextlib.ExitStack() as pes:
        sb = lambda name, shape, dt=F32: k.sb(pes, name, shape, dt)
        c["hbf"] = sb("hbf5", [128, 4, DM], BF16)
        c["hT"] = (sb("hT5", [128, 8, 512], BF16)[0], [Buf("hT5_%d" % i) for i in range(8)])
        aT, _ = sb("aT5", [128, NFC, 512], BF16)
        c["aT"] = (aT, [Buf("aT5_%d" % i) for i in range(NFC)])
        c["sg"] = [sb("sg50", [128, 512], BF16), sb("sg51", [128, 512], BF16)]
        c["ss"] = sb("ss5", [128, 4])
        xt, Bxt = sb("xt5", [128, 4, DM])
        wd, Bwd = sb("wd5", [128, NFC, DM], BF16)
        ws = WStream(k, [sb("wb5%d" % i, [128, 8, 512], BF16) for i in range(3)])
        woa, Bwoa = sb("woa", [128, 4, DM], BF16)
        wob, Bwob = sb("wob", [128, 4, DM], BF16)
        wout, Bwout = sb("wout", [128, 8, DM], BF16)
        ya, Bya = sb("ya5", [128, 4, 512], BF16)
        yb, Byb = sb("yb5", [128, 4, 512], BF16)
        gab = [sb("ga5%d" % i, [128, 512], BF16) for i in range(2)]
        gbb = [sb("gb5%d" % i, [128, 512], BF16) for i in range(2)]
        m1 = [sb("m1_%d" % i, [128, 512]) for i in range(2)]
        m2 = [sb("m2_%d" % i, [128, 512]) for i in range(2)]
        mg, Bmg = sb("mg", [128, 8, 512], BF16)
        g2_bc, Bg2 = sb("g2_bc", [128, DM])
        S.dma("sp", DMA(g2_bc[:, :], I["g_ffn2"].to_broadcast([128, DM])), Bg2, writes=[Bg2])
        Wd_r = I["w2_down"].rearrange("(fc p) d -> p fc d", p=128)
        for q in range(0, NFC, 6):
            n = min(6, NFC - q)
            S.dma("pool", DMA(wd[:, q:q + n, :], Wd_r[:, q:q + n, :]), Bwd, accum=[Bwd])
        S.dma("pool", DMA(woa[:, :, :], I["w_o_a"].rearrange("(kc p) f -> p kc f", p=128)), Bwoa, writes=[Bwoa])
        S.dma("pool", DMA(wob[:, :, :], I["w_o_b"].rearrange("(kc p) f -> p kc f", p=128)), Bwob, writes=[Bwob])
        S.dma("pool", DMA(wout[:, :, :], I["w_out"].rearrange("(kc p) f -> p kc f", p=128)), Bwout, writes=[Bwout])
        Wg_r = I["w2_gate"].rearrange("(kc p) f -> p kc f", p=128)
        Wu_r = I["w2_up"].rearrange("(kc p) f -> p kc f", p=128)
        for tt in range(NTT):
            push_ffn_weights(ws, Wg_r, Wu_r)
        x1_r = D["x1"][0].rearrange("(b p) d -> p b d", p=128)
        out_r = out.rearrange("(b p) d -> p b d", p=128)
        for tt in range(NTT):
            t0 = tt * 512
            S.dma("sp", DMA(xt[:, :, :], x1_r[:, tt * 4:(tt + 1) * 4, :]), Bxt, reads=[D["x1"][1]], writes=[Bxt])
            S.dma("sp", DMA(ya[:, :, :], D["yaT"][0][:, :, t0:t0 + 512].rearrange("a p t -> p a t")), Bya, reads=[D["yaT"][1]], writes=[Bya])
            S.dma("sp", DMA(yb[:, :, :], D["ybT"][0][:, :, t0:t0 + 512].rearrange("a p t -> p a t")), Byb, reads=[D["ybT"][1]], writes=[Byb])
            for mc in range(8):
                ga, Bga = gab[mc % 2]
                gb, Bgb = gbb[mc % 2]
                S.dma("sp", DMA(ga[:, :], D["gaT"][0][mc * 128:(mc + 1) * 128, t0:t0 + 512]), Bga, reads=[D["gaT"][1]], writes=[Bga])
                S.dma("sp", DMA(gb[:, :], D["gbT"][0][mc * 128:(mc + 1) * 128, t0:t0 + 512]), Bgb, reads=[D["gbT"][1]], writes=[Bgb])
                pa, Bpa = k.bank()
                pb, Bpb = k.bank()
                for pr in range(4):
                    S.add("pe", MM(pa[:, :], woa[:, pr, mc * 128:(mc + 1) * 128], ya[:, pr, :], pr == 0, pr == 3),
                          reads=[Bwoa, Bya], writes=[Bpa] if pr == 0 else (), accum=() if pr == 0 else [Bpa])
                for pr in range(4):
                    S.add("pe", MM(pb[:, :], wob[:, pr, mc * 128:(mc + 1) * 128], yb[:, pr, :], pr == 0, pr == 3),
                          reads=[Bwob, Byb], writes=[Bpb] if pr == 0 else (), accum=() if pr == 0 else [Bpb])
                a1, Ba1 = m1[mc % 2]
                a2, Ba2 = m2[mc % 2]
                S.add("dve", TT(a1[:, :], ga[:, :], pa[:, :], ALU.mult), reads=[Bga, Bpa], writes=[Ba1])
                S.add("dve", TT(a2[:, :], gb[:, :], pb[:, :], ALU.mult), reads=[Bgb, Bpb], writes=[Ba2])
                S.add("pool", TT(mg[:, mc, :], a1[:, :], a2[:, :], ALU.add), reads=[Ba1, Ba2], writes=[Bmg])
            for j in range(4):
                for half in range(2):
                    po, Bpo = k.bank()
                    for mc in range(8):
                        S.add("pe", MM(po[:, :], mg[:, mc, j * 128:(j + 1) * 128], wout[:, mc, half * 512:(half + 1) * 512], mc == 0, mc == 7),
                              reads=[Bmg, Bwout], writes=[Bpo] if mc == 0 else (), accum=() if mc == 0 else [Bpo])
                    sl = xt[:, j, half * 512:(half + 1) * 512]
                    S.add("dve", TT(sl, sl, po[:, :], ALU.add), reads=[Bpo, Bxt], writes=[Bxt])
            emit_ffn(k, c, xt, Bxt, g2_bc, Bg2, ws, wd, Bwd)
            S.dma("sp", DMA(out_r[:, tt * 4:(tt + 1) * 4, :], xt[:, :, :]), Bxt, reads=[Bxt])
        return [Bxt]
```
